# Optimizing a Trainium2 kernel written in Bass

```python
import jax, jax.numpy as jnp
from jax import lax
import numpy as np

D_MODEL = 1024
BATCH = 8
SEQ = 2048
DEPTH = 2
DEC_BATCH = 128
DEC_SEQ = 1
PAST_LEN = 16384
PAGE_SIZE = 128

N_MIXERS = 2
N_HGRN = (DEPTH + 1) // 2
N_POOL = DEPTH // 2
HGRN_EXPAND = 128
HGRN_HEADS = D_MODEL // HGRN_EXPAND
HGRN_DK = HGRN_EXPAND
HGRN_DV = D_MODEL // HGRN_HEADS
HGRN_KTOT = HGRN_HEADS * HGRN_DK
HGRN_VTOT = HGRN_HEADS * HGRN_DV
HGRN_CHUNK = 32
POOL_WINDOWS = (2, 4, 8, 16)
POOL_GROUPS = len(POOL_WINDOWS)
POOL_GW = D_MODEL // POOL_GROUPS
POOL_HIST = max(POOL_WINDOWS) - 1
D_FF = 4 * D_MODEL
EPS = 1e-6

kernel_name = "hgrn2_pool_interleaved_decode_step"


def _rmsnorm(x, g):
    xf = x.astype(jnp.float32)
    y = xf * lax.rsqrt(jnp.mean(xf * xf, axis=-1, keepdims=True) + EPS) * g.astype(jnp.float32)
    return y.astype(x.dtype)


def _gated_recurrence(q, k, v, g, s0):
    B, L, H, _ = q.shape
    C = min(HGRN_CHUNK, L)
    n = -(-L // C)
    Lp = n * C

    def chunks(a):
        a = jnp.pad(a, ((0, 0), (0, Lp - L), (0, 0), (0, 0)))
        return a.reshape(B, n, C, H, a.shape[-1]).swapaxes(0, 1)

    causal = jnp.tril(jnp.ones((C, C), dtype=bool))[None, :, :, None, None]

    def step(S, blk):
        qc, kc, vc, gc = blk
        G = jnp.cumsum(gc, axis=1)
        diff = G[:, :, None] - G[:, None, :]
        decay = jnp.exp(jnp.where(causal, diff, -jnp.inf))
        A = jnp.einsum('bthd,bshd,btshd->btsh', qc, kc, decay)
        o = (jnp.einsum('btsh,bshv->bthv', A, vc)
             + jnp.einsum('bthd,bhdv->bthv', qc * jnp.exp(G), S))
        G_last = G[:, -1]
        S = (jnp.exp(G_last)[..., None] * S
             + jnp.einsum('bshd,bshv->bhdv', kc * jnp.exp(G_last[:, None] - G), vc))
        return S, o

    S, o = lax.scan(step, s0, (chunks(q), chunks(k), chunks(v), chunks(g)))
    o = o.swapaxes(0, 1).reshape(B, Lp, H, v.shape[-1])[:, :L]
    return o, S


def _hgrn2(x, s0, norm_g, w_in, lb, onorm_g, w_out):
    B, L, _ = x.shape
    u = _rmsnorm(x, norm_g)
    proj = jnp.einsum('bld,de->ble', u, w_in).astype(jnp.float32)
    q, fz, inp, og = jnp.split(proj, [HGRN_KTOT, 2 * HGRN_KTOT, 2 * HGRN_KTOT + HGRN_VTOT], axis=-1)
    q = jax.nn.silu(q) * (HGRN_DK ** -0.5)
    f = lb + (1.0 - lb) * jax.nn.sigmoid(fz)
    k = 1.0 - f
    logf = jnp.log(f)
    hd = lambda a, d: a.reshape(B, L, HGRN_HEADS, d)
    o, s_new = _gated_recurrence(hd(q, HGRN_DK), hd(k, HGRN_DK), hd(inp, HGRN_DV),
                                 hd(logf, HGRN_DK), s0.astype(jnp.float32))
    o = o * lax.rsqrt(jnp.mean(o * o, axis=-1, keepdims=True) + EPS)
    o = o.reshape(B, L, HGRN_VTOT) * onorm_g.astype(jnp.float32) * jax.nn.sigmoid(og)
    y = jnp.einsum('ble,ed->bld', o.astype(x.dtype), w_out)
    return y, s_new.astype(s0.dtype)


def _pool(x, hist, start, norm_g, w_grp, scale):
    B, L, _ = x.shape
    u = _rmsnorm(x, norm_g)
    ext = jnp.concatenate([hist.astype(jnp.float32), u.astype(jnp.float32)], axis=1)
    cs = jnp.concatenate([jnp.zeros((B, 1, D_MODEL), jnp.float32), jnp.cumsum(ext, axis=1)], axis=1)
    P = POOL_HIST
    pos = start + jnp.arange(L)
    parts = []
    for gi, w in enumerate(POOL_WINDOWS):
        sl = slice(gi * POOL_GW, (gi + 1) * POOL_GW)
        hi = cs[:, P + 1:P + 1 + L, sl]
        lo = cs[:, P + 1 - w:P + 1 - w + L, sl]
        cnt = jnp.minimum(w, pos + 1).astype(jnp.float32)[None, :, None]
        parts.append((hi - lo) / cnt)
    pooled = jnp.concatenate(parts, axis=-1)
    z = (pooled - u.astype(jnp.float32)).reshape(B, L, POOL_GROUPS, POOL_GW)
    y = jnp.einsum('blgc,gcd->blgd', z.astype(x.dtype), w_grp).reshape(B, L, D_MODEL) * scale
    return y.astype(x.dtype), ext[:, -P:].astype(hist.dtype)


def _mlp(x, norm_g, w_up, w_down):
    h = _rmsnorm(x, norm_g)
    a = jnp.square(jax.nn.relu(jnp.einsum('bld,df->blf', h, w_up)))
    return jnp.einsum('blf,fd->bld', a, w_down)


def _trunk(x, s_hgrn, hist_pool, start, lbs, params):
    (hgrn_norm, hgrn_w_in, hgrn_onorm, hgrn_w_out, pool_norm, pool_w, pool_scale,
     mlp_norm, mlp_up, mlp_down, final_norm) = params
    new_h, new_p = [], []
    for layer in range(DEPTH):
        j = layer // N_MIXERS
        if layer % N_MIXERS == 0:
            y, s = _hgrn2(x, s_hgrn[j], hgrn_norm[j], hgrn_w_in[j], lbs[j], hgrn_onorm[j], hgrn_w_out[j])
            new_h.append(s)
        else:
            y, h = _pool(x, hist_pool[j], start, pool_norm[j], pool_w[j], pool_scale[j])
            new_p.append(h)
        x = x + y
        x = x + _mlp(x, mlp_norm[layer], mlp_up[layer], mlp_down[layer])
    return _rmsnorm(x, final_norm), jnp.stack(new_h), jnp.stack(new_p)


def setup_inputs(seed: int = 0) -> dict:
    key = jax.random.key(seed)
    ks = jax.random.split(key, 20)
    nrm = lambda k, shape, s: jax.random.normal(k, shape, jnp.float32) * s
    gain = lambda k, shape: 1.0 + nrm(k, shape, 0.05)
    return {
        "x_prompt": nrm(ks[0], (BATCH, SEQ, D_MODEL), 1.0),
        "x_sample": nrm(ks[1], (DEC_BATCH, DEC_SEQ, D_MODEL), 1.0),
        "state_hgrn": nrm(ks[2], (N_HGRN, DEC_BATCH, HGRN_HEADS, HGRN_DK, HGRN_DV), 0.5),
        "state_pool": nrm(ks[3], (N_POOL, DEC_BATCH, POOL_HIST, D_MODEL), 1.0),
        "hgrn_norm": gain(ks[4], (N_HGRN, D_MODEL)),
        "hgrn_w_in": nrm(ks[5], (N_HGRN, D_MODEL, 2 * HGRN_KTOT + 2 * HGRN_VTOT), D_MODEL ** -0.5),
        "hgrn_lb": nrm(ks[6], (N_HGRN + 1, HGRN_KTOT), 0.1),
        "hgrn_onorm": gain(ks[7], (N_HGRN, HGRN_VTOT)),
        "hgrn_w_out": nrm(ks[8], (N_HGRN, HGRN_VTOT, D_MODEL), HGRN_VTOT ** -0.5),
        "pool_norm": gain(ks[9], (N_POOL, D_MODEL)),
        "pool_w": nrm(ks[10], (N_POOL, POOL_GROUPS, POOL_GW, POOL_GW), POOL_GW ** -0.5),
        "pool_scale": gain(ks[11], (N_POOL, D_MODEL)),
        "mlp_norm": gain(ks[12], (DEPTH, D_MODEL)),
        "mlp_up": nrm(ks[13], (DEPTH, D_MODEL, D_FF), D_MODEL ** -0.5),
        "mlp_down": nrm(ks[14], (DEPTH, D_FF, D_MODEL), D_FF ** -0.5),
        "final_norm": gain(ks[15], (D_MODEL,)),
    }


def reference(x_prompt, x_sample, state_hgrn, state_pool, hgrn_norm, hgrn_w_in, hgrn_lb, hgrn_onorm,
              hgrn_w_out, pool_norm, pool_w, pool_scale, mlp_norm, mlp_up, mlp_down, final_norm):
    lbs = jnp.cumsum(jax.nn.softmax(hgrn_lb.astype(jnp.float32), axis=0), axis=0)[:N_HGRN]
    params = (hgrn_norm, hgrn_w_in, hgrn_onorm, hgrn_w_out, pool_norm, pool_w, pool_scale,
              mlp_norm, mlp_up, mlp_down, final_norm)
    b = x_prompt.shape[0]
    s0 = jnp.zeros((N_HGRN, b, HGRN_HEADS, HGRN_DK, HGRN_DV), state_hgrn.dtype)
    h0 = jnp.zeros((N_POOL, b, POOL_HIST, D_MODEL), state_pool.dtype)
    y_prompt, hgrn_p, pool_p = _trunk(x_prompt, s0, h0, 0, lbs, params)
    y_sample, hgrn_s, pool_s = _trunk(x_sample, state_hgrn, state_pool, PAST_LEN, lbs, params)
    return (y_prompt, y_sample, hgrn_p, hgrn_s, pool_p, pool_s)
```

```python
import math
from contextlib import ExitStack
import numpy as np
import concourse.bass as bass
import concourse.mybir as mybir
from concourse.bass_utils import run_bass_kernel_spmd

F32 = mybir.dt.float32
BF16 = mybir.dt.bfloat16
I32 = mybir.dt.int32
AF = mybir.ActivationFunctionType
ALU = mybir.AluOpType
AX = mybir.AxisListType

NCORES = 8
D = 1024
L = 2048
NS = 16
SEGP = 1024
NCOL = SEGP + NS
EPS = 1e-6
RING = 8
NG = 16
LN_QSCALE = math.log(128 ** -0.5)
WINDOWS = (2, 4, 8, 16)


class Res:
    __slots__ = ("name", "w", "r", "const", "excl")

    def __init__(self, name, const=False, excl=False):
        self.name = name
        self.w = None
        self.r = {}
        self.const = const
        self.excl = excl


class Op:
    __slots__ = ("eng", "fn", "deps", "chan", "ticket", "semval", "needed", "uid")


ENGS = ("pe", "act", "dve", "pool", "sp")


class Sched:
    def __init__(self):
        self.q = {e: [] for e in ENGS}
        self.ops = []

    def add(self, eng, fn, reads=(), writes=(), chan=None):
        op = Op()
        op.eng = eng
        op.fn = fn
        op.chan = chan
        op.ticket = None
        op.semval = None
        op.needed = False
        op.uid = len(self.ops)
        deps = set()
        for r in reads:
            if r.w is not None:
                deps.add(r.w)
            if r.excl:
                for o in r.r.values():
                    if o.eng != eng:
                        deps.add(o)
        for r in writes:
            if r.w is not None:
                deps.add(r.w)
            for o in r.r.values():
                deps.add(o)
        key = eng if chan is None else ("dma", op.uid)
        for r in reads:
            if not r.const:
                r.r[key] = op
        for r in writes:
            r.w = op
            r.r = {}
        deps.discard(op)
        op.deps = deps
        self.q[eng].append(op)
        self.ops.append(op)
        return op

    def emit(self, nc, stack, block):
        for op in self.ops:
            for d in op.deps:
                if d.chan is None and not (d.eng == "pe" and op.eng == "pe"):
                    d.needed = True
        for e in ENGS:
            cnt = 0
            for op in self.q[e]:
                if op.chan is None and op.needed:
                    cnt += 1
                    op.ticket = cnt
        chan_cnt = {}
        chan_sem = {}
        for op in self.ops:
            if op.chan is not None:
                chan_cnt[op.chan] = chan_cnt.get(op.chan, 0) + 16
                op.semval = chan_cnt[op.chan]
                if op.chan not in chan_sem:
                    chan_sem[op.chan] = stack.enter_context(nc.semaphore("c_" + op.chan))
        esem = {e: stack.enter_context(nc.semaphore("e_" + e)) for e in ENGS}

        def run(e, handle):
            seen = {}
            for op in self.q[e]:
                waits = {}
                for d in op.deps:
                    if d.chan is not None:
                        k = ("c", d.chan)
                        v = d.semval
                    else:
                        if d.eng == "pe" and e == "pe":
                            continue
                        k = ("e", d.eng)
                        v = d.ticket
                    if v > waits.get(k, 0):
                        waits[k] = v
                for k, v in waits.items():
                    if seen.get(k, 0) >= v:
                        continue
                    seen[k] = v
                    sem = chan_sem[k[1]] if k[0] == "c" else esem[k[1]]
                    handle.wait_ge(sem, v)
                if op.fn is None:
                    continue
                ins = op.fn(handle)
                if op.chan is not None:
                    ins.then_inc(chan_sem[op.chan], 16)
                elif op.needed:
                    ins.then_inc(esem[e], 1)

        @block.tensor
        def _(h):
            run("pe", h)

        @block.scalar
        def _(h):
            run("act", h)

        @block.vector
        def _(h):
            run("dve", h)

        @block.gpsimd
        def _(h):
            run("pool", h)

        @block.sync
        def _(h):
            run("sp", h)


def build():
    nc = bass.Bass("TRN2", target_bir_lowering=False)
    S = Sched()

    def din(name, shape):
        return nc.dram_tensor(name, list(shape), F32, kind="ExternalInput").ap()

    def dout(name, shape):
        return nc.dram_tensor(name, list(shape), F32, kind="ExternalOutput").ap()

    xp = din("xp", [L, D])
    xs = din("xs", [NS, D])
    sh = din("sh", [NS, 8, 128, 128])
    spl = din("spl", [NS, 15, D])
    hgrn_norm = din("hgrn_norm", [8, 128])
    w_in = din("w_in", [D, 4 * D])
    hgrn_lb = din("hgrn_lb", [16, 128])
    hgrn_onorm = din("hgrn_onorm", [8, 128])
    w_out = din("w_out", [D, D])
    pool_norm = din("pool_norm", [8, 128])
    pool_w = din("pool_w", [4, 256, 256])
    pool_scale = din("pool_scale", [8, 128])
    mlp_norm = din("mlp_norm", [16, 128])
    mlp_up = din("mlp_up", [2, D, 4 * D])
    mlp_down = din("mlp_down", [2, 4 * D, D])
    final_norm = din("final_norm", [8, 128])

    yp = dout("yp", [L, D])
    ys = dout("ys", [NS, D])
    hp = dout("hp", [8, 128, 128])
    hs = dout("hs", [NS, 8, 128, 128])
    pp = dout("pp", [15, D])
    pps = dout("pps", [NS, 15, D])

    stack = ExitStack()
    with stack:
        def sb(name, shape, dt):
            return stack.enter_context(nc.sbuf_tensor(name, list(shape), dt))

        xT = sb("xT", [128, 8, NCOL], F32)
        uT = sb("uT", [128, 8, NCOL], BF16)
        BIG = sb("BIG", [128, 17536], BF16)
        ring = sb("ring", [128, RING, 8, 128], BF16)
        stage = sb("stage", [128, 2, D], F32)
        xstage = sb("xstage", [128, 3, D], F32)
        gt = sb("gt", [128, NG, 512], F32)
        kTt = sb("kTt", [128, 2, 512], BF16)
        kLt = sb("kLt", [128, 2, 512], BF16)
        qTt = sb("qTt", [128, 2, 512], BF16)
        qSt = sb("qSt", [128, 3, 512], BF16)
        cst = sb("cst", [128, 3, 12], F32)
        dft = sb("dft", [128, 4, 12], F32)
        gmt = sb("gmt", [128, 4, 4], F32)
        am = sb("am", [128, 2, 512], BF16)
        ktok = sb("ktok", [128, 1, 512], BF16)
        Sbt = sb("Sbt", [128, 9, 128], BF16)
        Sbig = sb("Sbig", [128, 2, NS, 128], F32)
        kmask = sb("kmask", [16, NS, 128], BF16)
        sqn = sb("sqn", [128, 2, 512], BF16)
        Sst = sb("Sst", [128, 8, 128], F32)
        Sb16 = sb("Sb16", [128, 2, 4, 128], BF16)
        smp = sb("smp", [128, 16, NS], F32)
        qS = sb("qS", [128, 2, NS], BF16)
        histsave = sb("histsave", [128, 8, 15], F32)
        ident = sb("ident", [128, 128], F32)
        identb = sb("identb", [128, 128], BF16)
        onesf = sb("onesf", [128, 512], BF16)
        onesb = sb("onesb", [128, 128], BF16)
        mask4 = sb("mask4", [128, 4, 128], BF16)
        ktoks = sb("ktoks", [16, 128], F32)
        bandc = sb("bandc", [128, 12, 128], BF16)
        utokprev = sb("utokprev", [128, D], BF16)
        bandi = sb("bandi", [128, 128], I32)
        prow = sb("prow", [72, 128], F32)
        pcol = sb("pcol", [128, 72], F32)
        lbt = sb("lbt", [128, 5, 8], F32)
        invc = sb("invc", [128, 4, 16], F32)
        cnti = sb("cnti", [128, 16], I32)
        cntf = sb("cntf", [128, 16], F32)
        fsc = sb("fsc", [128, 1], F32)
        EPSB = sb("epsb", [128, 1], F32)
        ONEB = sb("oneb", [128, 1], F32)
        LNQ = sb("lnq", [128, 1], F32)
        ups = sb("ups", [128, 8, NS], F32)
        reds = sb("reds", [128, NS], F32)
        ps = stack.enter_context(nc.psum_tensor("ps", [128, 8, 512], F32))
        block = stack.enter_context(nc.Block())

        vtok = BIG[:, 0:9216].rearrange("p (j e) -> p j e", e=1024)
        oT = BIG[:, 9216:9216 + 8320].rearrange("p (h c) -> p h c", c=NCOL)
        hT = BIG[:, 0:16640].rearrange("p (m c) -> p m c", c=NCOL)
        bigf = BIG[:, 0:16640].bitcast(F32).rearrange("p (k c) -> p k c", c=NCOL)
        upT = bigf
        yT = bigf

        R_ps = [Res("ps%d" % b, excl=True) for b in range(8)]
        R_slot = [Res("slot%d" % s) for s in range(RING)]
        R_stage = [Res("stage%d" % i) for i in range(2)]
        R_xs = [Res("xstage%d" % i) for i in range(3)]
        R_gt = [Res("gt%d" % i) for i in range(NG)]
        R_x = {(k, t): Res("x%d_%d" % (k, t)) for k in range(8) for t in range(3)}
        R_u = {(k, t): Res("u%d_%d" % (k, t)) for k in range(8) for t in range(3)}
        R_v = {(g, j): Res("v%d_%d" % (g, j)) for g in range(2) for j in range(9)}
        R_o = {(h, t): Res("o%d_%d" % (h, t)) for h in range(8) for t in range(3)}
        R_h = {(m, t): Res("h%d_%d" % (m, t)) for m in range(16) for t in range(3)}
        R_up = {(k, t): Res("up%d_%d" % (k, t)) for k in range(8) for t in range(3)}
        R_uph = [Res("uph%d" % k) for k in range(8)]
        R_y = {(k, t): Res("y%d_%d" % (k, t)) for k in range(8) for t in range(3)}
        R_S = [Res("S%d" % h) for h in range(8)]
        R_kT = [Res("kT%d" % i) for i in range(4)]
        R_kL = [Res("kL%d" % i) for i in range(4)]
        R_sqpair = [Res("sqpair0"), Res("sqpair1")]
        R_qT = [Res("qT%d" % i) for i in range(4)]
        R_qSt = [Res("qSt%d" % i) for i in range(3)]
        R_lg = [Res("lg%d" % i) for i in range(4)]
        R_cs = [Res("cs%d" % i) for i in range(4)]
        R_df = [Res("df%d" % i) for i in range(4)]
        R_gm = [Res("gm%d" % i) for i in range(4)]
        R_am = [Res("am%d" % i) for i in range(2)]
        R_ktok = [Res("ktok%d" % i) for i in range(2)]
        R_Sb = [Res("Sb%d" % i) for i in range(9)]
        R_sqn = [Res("sqn%d" % i) for i in range(2)]
        R_Sbig = [[Res("Sbig%d_%d" % (i, g)) for g in range(4)] for i in range(2)]
        R_Sb16 = [Res("Sb16%d" % g) for g in range(4)]
        R_kmask = Res("kmask")
        R_utok = [Res("utok%d" % j) for j in range(8)]
        R_utokprev = Res("utokprev")
        R_band = Res("band", const=True)
        R_ktoks = Res("ktoks")
        R_smp = [Res("smp%d" % i) for i in range(16)]
        R_qS = [Res("qS%d" % i) for i in range(2)]
        R_hsave = Res("histsave")
        R_const = Res("const", const=True)
        R_prow = Res("prow")
        R_pcol = Res("pcol", const=True)
        R_lb = Res("lb", const=True)
        R_misc = Res("misc")
        R_ones = Res("ones", const=True)
        R_fsc = Res("fsc")
        out_dma_ops = []

        def act(out, in_, func, R, W, bias=None, scale=None):
            kw = {}
            if bias is not None:
                kw["bias"] = bias
            if scale is not None:
                kw["scale"] = scale
            return S.add("act", lambda e: e.activation(out=out, in_=in_, func=func, **kw), R, W)

        def tt(eng, out, in0, in1, op, R, W):
            return S.add(eng, lambda e: e.tensor_tensor(out=out, in0=in0, in1=in1, op=op), R, W)

        def ts(eng, out, in0, s1, op0, R, W, s2=None, op1=None):
            if op1 is None:
                return S.add(eng, lambda e: e.tensor_scalar(out=out, in0=in0, scalar1=s1, scalar2=None, op0=op0), R, W)
            return S.add(eng, lambda e: e.tensor_scalar(out=out, in0=in0, scalar1=s1, scalar2=s2, op0=op0, op1=op1), R, W)

        def stt(out, in0, scalar, in1, op0, op1, R, W):
            return S.add("dve", lambda e: e.scalar_tensor_tensor(out=out, in0=in0, scalar=scalar, in1=in1, op0=op0, op1=op1), R, W)

        def cp(eng, out, in_, R, W):
            if eng == "act":
                return S.add("act", lambda e: e.copy(out=out, in_=in_), R, W)
            return S.add(eng, lambda e: e.tensor_copy(out=out, in_=in_), R, W)

        def mm(out, lhsT, rhs, start, stop, R, W):
            return S.add("pe", lambda e: e.matmul(out, lhsT=lhsT, rhs=rhs, start=start, stop=stop, skip_group_check=True), R, W)

        def tr(out, in_, idn, R, W):
            return S.add("pe", lambda e: e.transpose(out, in_, idn), R, W)

        def dma(q, out, in_, R, W, chan, is_out=False):
            op = S.add(q, lambda e: e.dma_start(out=out, in_=in_), R, W, chan=chan)
            if is_out:
                out_dma_ops.append(op)
            return op

        psum_i = [0]
        psum_free_list = list(range(8))
        psum_last = [0] * 8

        def psum_next():
            b = psum_alloc()
            psum_release(b)
            return b

        def psum_alloc():
            assert psum_free_list, "out of PSUM banks"
            b = min(psum_free_list, key=lambda i: psum_last[i])
            psum_free_list.remove(b)
            psum_i[0] += 1
            psum_last[b] = psum_i[0]
            return b

        def psum_release(b):
            psum_i[0] += 1
            psum_last[b] = psum_i[0]
            psum_free_list.append(b)

        def fence(old, new):
            op = S.add("dve", lambda e: e.memset(fsc[:], 0.0), [], list(old) + [R_fsc])
            for r in new:
                r.w = op
                r.r = {}

        def wpiece(ap2d, nk=8):
            return (ap2d.rearrange("(kc p) e -> p kc e", p=128), nk)

        pieces = []
        for seg in range(2):
            for h in range(8):
                pieces.append(wpiece(w_in[:, 2048 + h * 128: 2048 + (h + 1) * 128]))
            for h in range(8):
                for off in (0, 1024, 3072):
                    pieces.append(wpiece(w_in[:, off + h * 128: off + (h + 1) * 128]))
            for m in range(8):
                pieces.append(wpiece(w_out[:, m * 128:(m + 1) * 128]))
            for l in range(2):
                if l == 1:
                    for m in range(8):
                        g = m // 2
                        pieces.append(wpiece(pool_w[g][:, (m % 2) * 128:(m % 2 + 1) * 128], 2))
                for fb in range(2):
                    for m in range(16):
                        c0 = (fb * 16 + m) * 128
                        pieces.append(wpiece(mlp_up[l][:, c0:c0 + 128]))
                    for m in range(8):
                        for half in range(2):
                            r0 = fb * 2048 + half * 1024
                            pieces.append(wpiece(mlp_down[l][r0:r0 + 1024, m * 128:(m + 1) * 128]))
        wstate = {"next_load": 0, "next_get": 0}

        def w_issue(slot):
            i = wstate["next_load"]
            if i >= len(pieces):
                return
            wstate["next_load"] += 1
            ap, nk = pieces[i]
            dma("pool", ring[:, slot, 0:nk, :], ap, [], [R_slot[slot]], "w%d" % slot)

        def w_get():
            i = wstate["next_get"]
            wstate["next_get"] += 1
            return i % RING

        def w_release(slot):
            w_issue(slot)

        S.add("pool", lambda e: e.memset(onesf[:], 1.0), [], [R_ones])
        S.add("pool", lambda e: e.affine_select(out=ident[:], in_=onesf[:, 0:128], pattern=[[1, 128]],
                                                 compare_op=ALU.is_equal, fill=0.0, base=0, channel_multiplier=-1),
              [R_ones], [R_const])
        S.add("pool", lambda e: e.affine_select(out=identb[:], in_=onesf[:, 0:128], pattern=[[1, 128]],
                                                 compare_op=ALU.is_equal, fill=0.0, base=0, channel_multiplier=-1),
              [R_ones], [R_const])
        S.add("pool", lambda e: e.affine_select(out=mask4[:], in_=onesf[:].rearrange("p (c j) -> p c j", j=128),
                                                 pattern=[[0, 4], [1, 128]], compare_op=ALU.is_ge, fill=0.0, base=0,
                                                 channel_multiplier=-1), [R_ones], [R_const])
        S.add("pool", lambda e: e.memset(onesb[:], 1.0), [], [R_const])
        S.add("pool", lambda e: e.iota(cnti[:], pattern=[[1, 16]], base=1, channel_multiplier=0), [], [R_misc])
        cp("dve", cntf[:], cnti[:], [R_misc], [R_misc])
        for wi, w in enumerate(WINDOWS):
            ts("dve", invc[:, wi, :], cntf[:], float(w), ALU.min, [R_misc], [R_misc])
        S.add("dve", lambda e: e.reciprocal(out=invc[:], in_=invc[:]), [R_misc], [R_misc])
        for h in range(8):
            S.add("dve", lambda e, h=h: e.memset(Sst[:, h, :], 0.0), [], [R_S[h]])
        S.add("dve", lambda e: e.memset(histsave[:], 0.0), [], [R_hsave])

        for s in range(RING):
            w_issue(s)

        def seg_tiles(seg):
            tl = [(0, 0, 512), (1, 512, 512)]
            if seg == 0:
                tl.append((2, 1024, NS))
            return tl

        def x_chunks(seg):
            chunks = [(j, xp[seg * SEGP + j * 128: seg * SEGP + (j + 1) * 128, :], 128) for j in range(8)]
            if seg == 0:
                chunks.append((8, xs, NS))
            return chunks

        xl = {"n": 0, "buf": {}, "bufs": None}

        def issue_load(seg, ci):
            chunks = x_chunks(seg)
            if ci >= len(chunks):
                return
            (j, src, rows) = chunks[ci]
            bufd = xl["bufs"][xl["n"] % len(xl["bufs"])]
            xl["n"] += 1
            xl["buf"][(seg, ci)] = bufd
            dma("sp", bufd[0][0:rows, bufd[1], :], src, [], [bufd[2]], bufd[3])

        def load_chunk(seg, ci):
            (j, src, rows) = x_chunks(seg)[ci]
            xbuf, sg, Rxb, _ = xl["buf"][(seg, ci)]
            t = j // 4 if j < 8 else 2
            c0 = j * 128
            for half in range(2):
                b = psum_next()
                for kk in range(4):
                    k = half * 4 + kk
                    tr(ps[:, b, kk * 128: kk * 128 + rows], xbuf[0:rows, sg, k * 128:(k + 1) * 128],
                       ident[0:rows, 0:rows], [Rxb, R_const], [R_ps[b]])
                eng = "act" if half == 0 else "dve"
                src_ap = ps[:, b, :].rearrange("p (k c) -> p k c", c=128)[:, :, 0:rows]
                cp(eng, xT[:, half * 4: half * 4 + 4, c0:c0 + rows], src_ap,
                   [R_ps[b]], [R_x[(half * 4 + kk, t)] for kk in range(4)])

        def rmsnorm(seg, gcol, dst, R_dst, dst_col_off=0):
            for (t, c0, n) in seg_tiles(seg):
                b = psum_next()
                for q4 in range(2):
                    k0 = 4 * q4
                    g0 = 2 + 2 * q4
                    sqb = gt[:, g0:g0 + 2, :].rearrange("p a b -> p (a b)").bitcast(BF16).rearrange("p (k c) -> p k c", c=512)
                    Rsq = [R_gt[g0], R_gt[g0 + 1]]
                    act(sqb[:, :, 0:n], xT[:, k0:k0 + 4, c0:c0 + n], AF.Square, [R_x[(k0 + i, t)] for i in range(4)], Rsq)
                    for i in range(4):
                        k = k0 + i
                        mm(ps[:, b, 0:n], onesb[:], sqb[:, i, 0:n], k == 0, k == 7, [R_const] + Rsq, [R_ps[b]])
                gi = (12, 13, 6)[t]
                rstd = gt[:, gi, 0:n]
                act(rstd, ps[:, b, 0:n], AF.Ln, [R_ps[b]], [R_gt[gi]], bias=EPSB[:], scale=1.0 / D)
                act(rstd, rstd, AF.Exp, [R_gt[gi]], [R_gt[gi]], scale=-0.5)
                for k in range(8):
                    stt(dst[:, k, dst_col_off + c0: dst_col_off + c0 + n], xT[:, k, c0:c0 + n], pcol[:, gcol + k: gcol + k + 1],
                        rstd, ALU.mult, ALU.mult, [R_x[(k, t)], R_gt[gi], R_pcol], [R_dst[(k, t)]])

        S.add("dve", lambda e: e.memset(EPSB[:], EPS), [], [R_const])
        S.add("dve", lambda e: e.memset(ONEB[:], 1.0), [], [R_const])
        S.add("dve", lambda e: e.memset(LNQ[:], LN_QSCALE), [], [R_const])

        def vproj(seg):
            chunks = list(range(8)) + ([8] if seg == 0 else [])
            for hg in range(2):
                slots = [w_get() for _ in range(4)]
                for j in chunks:
                    rows = 128 if j < 8 else NS
                    t = j // 4 if j < 8 else 2
                    c0 = j * 128
                    b = psum_next()
                    for hh in range(4):
                        sl = slots[hh]
                        for k in range(8):
                            mm(ps[0:rows, b, hh * 128:(hh + 1) * 128], uT[:, k, c0:c0 + rows], ring[:, sl, k, :],
                               k == 0, k == 7, [R_u[(k, t)], R_slot[sl]], [R_ps[b]])
                    eng = "act" if (j % 2 == 0) else "dve"
                    cp(eng, vtok[0:rows, j, hg * 512:(hg + 1) * 512], ps[0:rows, b, :], [R_ps[b]], [R_v[(hg, j)]])
                for sl in slots:
                    w_release(sl)

        GT_L1 = (0, 1)
        GT_L2 = (2, 3)
        GT_LQ = (4, 5)
        GT_QSB = (6, 7)
        GT_LG = (8, 9, 10, 15)
        GT_G, GT_EA, GT_EQ, GT_LNMS = 11, 12, 13, 14

        def norm_gate(h, t, c0, n, b_o, lg_ap, R_lg_res, si):
            act(sqn[:, si, 0:n], ps[:, b_o, 0:n], AF.Square, [R_ps[b_o]], [R_sqn[si]])
            bn = psum_next()
            mm(ps[:, bn, 0:n], onesb[:], sqn[:, si, 0:n], True, True, [R_const, R_sqn[si]], [R_ps[bn]])
            lnms = gt[:, GT_LNMS, 0:n]
            Rl = R_gt[GT_LNMS]
            act(lnms, ps[:, bn, 0:n], AF.Ln, [R_ps[bn]], [Rl], bias=EPSB[:], scale=1.0 / 128)
            stt(lnms, lnms, -0.5, lg_ap, ALU.mult, ALU.subtract, [Rl, R_lg_res], [Rl])
            act(lnms, lnms, AF.Exp, [Rl], [Rl])
            stt(oT[:, h, c0:c0 + n], ps[:, b_o, 0:n], pcol[:, C_ON + h: C_ON + h + 1], lnms, ALU.mult, ALU.mult,
                [R_ps[b_o], Rl, R_pcol], [R_o[(h, t)]])

        def proj3(u, n, c0, t):
            slq, slf, slo = u["slots"]
            Ru = [R_u[(k, t)] for k in range(8)]
            banks = []
            for sl in (slf, slq, slo):
                bk = psum_alloc()
                for k in range(8):
                    mm(ps[:, bk, 0:n], ring[:, sl, k, :], uT[:, k, c0:c0 + n], k == 0, k == 7, [Ru[k], R_slot[sl]], [R_ps[bk]])
                banks.append(bk)
            return banks

        def P_pe(u):
            u["bf"], u["bq"], u["bo"] = proj3(u, 512, u["c0"], u["t"])

        def P_act(u):
            h, p, n = u["h"], u["p"], 512
            bf, bq, bo = u["bf"], u["bq"], u["bo"]
            eA, eQ = gt[:, GT_EA, :], gt[:, GT_EQ, :]
            l1, l2, lq = gt[:, GT_L1[p], :], gt[:, GT_L2[p], :], gt[:, GT_LQ[p], :]
            lg = gt[:, GT_LG[u["ui"] % 4], :]
            act(eA, ps[:, bf, :], AF.Exp, [R_ps[bf]], [R_gt[GT_EA]], scale=-1.0)
            act(l1, eA, AF.Ln, [R_gt[GT_EA]], [R_gt[GT_L1[p]]], bias=ONEB[:])
            act(l2, eA, AF.Ln, [R_gt[GT_EA], R_lb], [R_gt[GT_L2[p]]], bias=ONEB[:], scale=LB(h))
            act(eQ, ps[:, bq, :], AF.Exp, [R_ps[bq]], [R_gt[GT_EQ]], scale=-1.0)
            act(lq, eQ, AF.Ln, [R_gt[GT_EQ]], [R_gt[GT_LQ[p]]], bias=ONEB[:])
            act(lg, ps[:, bo, :], AF.Exp, [R_ps[bo]], [R_gt[GT_LG[u["ui"] % 4]]], scale=-1.0)
            psum_release(bo)
            act(lg, lg, AF.Ln, [R_gt[GT_LG[u["ui"] % 4]]], [R_gt[GT_LG[u["ui"] % 4]]], bias=ONEB[:])

        def P_dve(u):
            p = u["p"]
            bf, bq = u["bf"], u["bq"]
            l1 = gt[:, GT_L1[p], :]
            k0 = gt[:, GT_QSB[p], :]
            stt(k0, ps[:, bf, :], -1.0, l1, ALU.mult, ALU.subtract, [R_ps[bf], R_gt[GT_L1[p]]], [R_gt[GT_QSB[p]]])
            psum_release(bf)

        def E1(u):
            p = u["p"]
            l1, l2, lq, k0 = gt[:, GT_L1[p], :], gt[:, GT_L2[p], :], gt[:, GT_LQ[p], :], gt[:, GT_QSB[p], :]
            R1, R2, RQ, RK, RG = R_gt[GT_L1[p]], R_gt[GT_L2[p]], R_gt[GT_LQ[p]], R_gt[GT_QSB[p]], R_gt[GT_G]
            G = gt[:, GT_G, :]
            S.add("dve", lambda e: e.tensor_tensor_scan(out=G, data0=l1, data1=l2, initial=0.0,
                                                        op0=ALU.add, op1=ALU.subtract), [R1, R2], [RG])
            G3 = G.rearrange("p (c j) -> p c j", j=128)
            Glast = G3[:, :, 127]
            gm = gmt[:, p, :]
            df = dft[:, p, :]
            Rdf = R_df[p]
            cp("dve", gm, G3[:, :, 63], [RG], [R_gm[p]])
            cp("dve", df[:, 0:1], gm[:, 0:1], [R_gm[p]], [Rdf])
            tt("dve", df[:, 1:4], gm[:, 1:4], gm[:, 0:3], ALU.subtract, [R_gm[p]], [Rdf])
            tt("dve", df[:, 4:8], Glast, gm, ALU.subtract, [RG, R_gm[p]], [Rdf])
            tt("pool", G3, G3, gm.unsqueeze(2).to_broadcast([128, 4, 128]), ALU.subtract, [RG, R_gm[p]], [RG])
            tt("pool", k0, k0, G, ALU.add, [RK, RG], [RK])
            tt("pool", lq, G, lq, ALU.add, [RG, RQ], [RQ])

        def E_act(u):
            h, p = u["h"], u["p"]
            k0, lq = gt[:, GT_QSB[p], :], gt[:, GT_LQ[p], :]
            c3 = u["ui"] % 3
            act(cst[:, c3, 0:8], dft[:, p, 0:8], AF.Exp, [R_df[p]], [R_cs[c3]], scale=-1.0)
            act(kTt[:, p, :], k0, AF.Exp, [R_gt[GT_QSB[p]], R_lb], [R_kT[p]], bias=LN1MLB(h))
            act(lq, lq, AF.Exp, [R_gt[GT_LQ[p]]], [R_gt[GT_LQ[p]]], bias=LNQ[:], scale=-1.0)

        def E2(u):
            p = u["p"]
            c3 = u["ui"] % 3
            bq = u["bq"]
            tt("dve", qTt[:, p, :], ps[:, bq, :], gt[:, GT_LQ[p], :], ALU.mult,
               [R_ps[bq], R_gt[GT_LQ[p]]], [R_qT[p]])
            psum_release(bq)
            m0 = cst[:, c3, 0:1]
            if u["t"] == 1:
                c3p = (u["ui"] - 1) % 3
                ts("dve", cst[:, c3, 8:10], cst[:, c3, 0:2], cst[:, c3p, 7:8], ALU.mult, [R_cs[c3], R_cs[c3p]], [R_cs[c3]])
                m0 = cst[:, c3, 8:9]
            qM3 = qSt[:, c3, :].rearrange("p (c j) -> p c j", j=128)
            qT3 = qTt[:, p, :].rearrange("p (c j) -> p c j", j=128)
            tt("pool", qM3[:, 1:4, :], qT3[:, 1:4, :], cst[:, c3, 1:4].unsqueeze(2).to_broadcast([128, 3, 128]), ALU.mult,
               [R_qT[p], R_cs[c3]], [R_qSt[c3]])
            ts("pool", qSt[:, c3, 0:128], qTt[:, p, 0:128], m0, ALU.mult, [R_qT[p], R_cs[c3]], [R_qSt[c3]], s2=0.0, op1=ALU.add)

        def R_early(u):
            h, t, p = u["h"], u["t"], u["p"]
            c3e = u["ui"] % 3
            ba = psum_alloc()
            for c in range(4):
                cs_ = slice(c * 128, (c + 1) * 128)
                mm(ps[:, ba, cs_], kTt[:, p, cs_], qTt[:, p, cs_], True, True, [R_kT[p], R_qT[p]], [R_ps[ba]])
            bt = psum_alloc()
            psb = ps[:, bt, :].bitcast(BF16)
            for c in range(4):
                cs_ = slice(c * 128, (c + 1) * 128)
                tr(psb[:, cs_], kTt[:, p, cs_], identb[:], [R_kT[p], R_const], [R_ps[bt]])
            u["ba"], u["bt"] = ba, bt

        def R_mid(u):
            h, t, c0 = u["h"], u["t"], u["c0"]
            hg = h // 4
            hc = slice(h * 128, (h + 1) * 128)
            ba, bt = u["ba"], u["bt"]
            q = u["p"]
            if t == 0:
                cp("pool", Sbt[:, 0, :], Sst[:, h, :], [R_S[h]], [R_Sb[0]])
            psb = ps[:, bt, :].bitcast(BF16)
            tt("dve", am[:, q, :], ps[:, ba, :], mask4[:].rearrange("p c j -> p (c j)"), ALU.mult,
               [R_ps[ba], R_const], [R_am[q]])
            psum_release(ba)
            cp("act", ktok[:, 0, :], psb[:, 0:512], [R_ps[bt]], [R_ktok[0]])
            psum_release(bt)
            bs = psum_alloc()
            for c in range(4):
                cs_ = slice(c * 128, (c + 1) * 128)
                j = (c0 // 128) + c
                mm(ps[:, bs, cs_], ktok[:, 0, cs_], vtok[:, j, hc], True, True, [R_ktok[0], R_v[(hg, j)]], [R_ps[bs]])
            u["bs"] = bs

        def R_chain(u):
            h, t, c0, p = u["h"], u["t"], u["c0"], u["p"]
            hg = h // 4
            hc = slice(h * 128, (h + 1) * 128)
            bs = u["bs"]
            q = u["p"]
            c3 = u["ui"] % 3
            b_o = psum_alloc()

            def mfac(c):
                if c == 0:
                    return cst[:, c3, 0:1] if t == 0 else cst[:, c3, 8:9]
                return cst[:, c3, c:c + 1]

            for c in range(4):
                cs_ = slice(c * 128, (c + 1) * 128)
                j = (c0 // 128) + c
                gc = t * 4 + c
                mm(ps[:, b_o, cs_], Sbt[:, gc, :], qSt[:, c3, cs_], True, False, [R_Sb[gc], R_qSt[c3]], [R_ps[b_o]])
                mm(ps[:, b_o, cs_], vtok[:, j, hc], am[:, q, cs_], False, True, [R_v[(hg, j)], R_am[q]], [R_ps[b_o]])
                stt(Sbt[:, gc + 1, :], Sst[:, h, :], mfac(c), ps[:, bs, cs_], ALU.mult, ALU.add,
                    [R_S[h], R_cs[c3], R_ps[bs]], [R_Sb[gc + 1]])
                stt(Sst[:, h, :], Sst[:, h, :], mfac(c), ps[:, bs, cs_], ALU.mult, ALU.add,
                    [R_S[h], R_cs[c3], R_ps[bs]], [R_S[h]])
            if t == 1:
                ts("dve", Sst[:, h, :], Sst[:, h, :], cst[:, c3, 7:8], ALU.mult, [R_S[h], R_cs[c3]], [R_S[h]])
            psum_release(bs)
            u["b_o"] = b_o

        def R_norm(u):
            li = GT_LG[u["ui"] % 4]
            norm_gate(u["h"], u["t"], u["c0"], 512, u["b_o"], gt[:, li, :], R_gt[li], 0)
            psum_release(u["b_o"])

        def SP_stage(u):
            h = u["h"]
            t, c0, n = 2, 1024, NS
            hp = h % 2
            for g4 in range(4):
                dma("sp", Sbig[:, hp, g4 * 4:(g4 + 1) * 4, :], sh[g4 * 4:(g4 + 1) * 4, h].rearrange("n d v -> d n v"), [],
                    [R_Sbig[hp][g4]], "sbig%d_%d" % (hp, g4))
            u["sbanks"] = proj3(u, n, c0, t)

        def SP_rest(u):
            h = u["h"]
            t, c0, n = 2, 1024, NS
            hp = h % 2
            bf, bq, bo = u["sbanks"]
            s0 = 8 * hp
            eA, l1, l2, bQ, fS, kS, sgq, lgs = [smp[:, s0 + i, :] for i in range(8)]
            RA, R1, R2, RQ, RfS, RkS, Rsg, Rlg = [R_smp[s0 + i] for i in range(8)]
            pf, pq, po = ps[:, bf, 0:n], ps[:, bq, 0:n], ps[:, bo, 0:n]
            act(eA, pf, AF.Exp, [R_ps[bf]], [RA], scale=-1.0)
            act(l1, eA, AF.Ln, [RA], [R1], bias=ONEB[:])
            act(l2, eA, AF.Ln, [RA, R_lb], [R2], bias=ONEB[:], scale=LB(h))
            act(bQ, pq, AF.Exp, [R_ps[bq]], [RQ], scale=-1.0)
            act(bQ, bQ, AF.Ln, [RQ], [RQ], bias=ONEB[:])
            act(lgs, po, AF.Exp, [R_ps[bo]], [Rlg], scale=-1.0)
            psum_release(bo)
            act(lgs, lgs, AF.Ln, [Rlg], [Rlg], bias=ONEB[:])
            tt("dve", l2, l2, l1, ALU.subtract, [R2, R1], [R2])
            stt(l1, pf, -1.0, l1, ALU.mult, ALU.subtract, [R_ps[bf], R1], [R1])
            psum_release(bf)
            act(fS, l2, AF.Exp, [R2], [RfS])
            act(kS, l1, AF.Exp, [R1, R_lb], [RkS], bias=LN1MLB(h))
            act(sgq, bQ, AF.Exp, [RQ], [Rsg], bias=LNQ[:], scale=-1.0)
            tt("dve", qS[:, hp, :], pq, sgq, ALU.mult, [R_ps[bq], Rsg], [R_qS[hp]])
            psum_release(bq)

        def SR_stage(u):
            h = u["h"]
            t, c0, n = 2, 1024, NS
            hp = h % 2
            hg = h // 4
            hc = slice(h * 128, (h + 1) * 128)
            s0 = 8 * hp
            fS, kS, lgs = smp[:, s0 + 4, :], smp[:, s0 + 5, :], smp[:, s0 + 7, :]
            RfS, RkS, Rlg = R_smp[s0 + 4], R_smp[s0 + 5], R_smp[s0 + 7]
            bk = psum_alloc()
            tr(ps[0:NS, bk, 0:128], kS, ident[:], [RkS, R_const], [R_ps[bk]])
            cp("dve", ktoks[:], ps[0:NS, bk, 0:128], [R_ps[bk]], [R_ktoks])
            S.add("pool", lambda e: e.affine_select(out=kmask[:], in_=ktoks[:].unsqueeze(1).to_broadcast([NS, NS, 128]),
                                                     pattern=[[1, NS], [0, 128]], compare_op=ALU.is_equal, fill=0.0, base=0,
                                                     channel_multiplier=-1), [R_ktoks], [R_kmask])
            psum_release(bk)
            for g4 in range(4):
                sl_ = slice(g4 * 4, (g4 + 1) * 4)
                tt("pool", Sbig[:, hp, sl_, :], Sbig[:, hp, sl_, :], fS[:, sl_].unsqueeze(2).to_broadcast([128, 4, 128]), ALU.mult,
                   [R_Sbig[hp][g4], RfS], [R_Sbig[hp][g4]])

        def SR_main(u):
            h = u["h"]
            t, c0, n = 2, 1024, NS
            hp = h % 2
            hg = h // 4
            hc = slice(h * 128, (h + 1) * 128)
            s0 = 8 * hp
            fS, kS, lgs = smp[:, s0 + 4, :], smp[:, s0 + 5, :], smp[:, s0 + 7, :]
            RfS, RkS, Rlg = R_smp[s0 + 4], R_smp[s0 + 5], R_smp[s0 + 7]
            b_os = psum_alloc()

            def kv_pair(pr):
                bl = []
                for gg in (2 * pr, 2 * pr + 1):
                    bb = psum_alloc()
                    bl.append(bb)
                    for i in range(4):
                        mm(ps[:, bb, i * 128:(i + 1) * 128], kmask[:, gg * 4 + i, :], vtok[0:NS, 8, hc], True, True,
                           [R_kmask, R_v[(hg, 8)]], [R_ps[bb]])
                return bl

            def upd_pair(pr, bl):
                for gi, gg in enumerate((2 * pr, 2 * pr + 1)):
                    m0 = gg * 4
                    bb2 = bl[gi]
                    tt("dve", Sbig[:, hp, m0:m0 + 4, :], ps[:, bb2, :].rearrange("p (n v) -> p n v", v=128),
                       Sbig[:, hp, m0:m0 + 4, :], ALU.add, [R_ps[bb2], R_Sbig[hp][gg]], [R_Sbig[hp][gg]])
                    psum_release(bb2)
                    cp("act", Sb16[:, gg % 2], Sbig[:, hp, m0:m0 + 4, :], [R_Sbig[hp][gg]], [R_Sb16[gg % 2]])

            def o_pair(pr):
                for gg in (2 * pr, 2 * pr + 1):
                    m0 = gg * 4
                    for i in range(4):
                        nn = m0 + i
                        mm(ps[:, b_os, nn:nn + 1], Sb16[:, gg % 2, i, :], qS[:, hp, nn:nn + 1], True, True,
                           [R_Sb16[gg % 2], R_qS[hp]], [R_ps[b_os]])

            bl0 = kv_pair(0)
            upd_pair(0, bl0)
            bl1 = kv_pair(1)
            o_pair(0)
            upd_pair(1, bl1)
            o_pair(1)
            for g4 in range(4):
                dma("sp", hs[g4 * 4:(g4 + 1) * 4, h].rearrange("n d v -> d n v"), Sbig[:, hp, g4 * 4:(g4 + 1) * 4, :],
                    [R_Sbig[hp][g4]], [], "sbo%d_%d" % (hp, g4), is_out=True)
            norm_gate(h, t, c0, n, b_os, lgs, Rlg, 1)
            psum_release(b_os)

        def hgrn_phase(seg):
            rmsnorm(seg, C_HN, uT, R_u)
            vproj(seg)
            units = []
            for h in range(8):
                units.append({"kind": "p", "h": h, "t": 0, "c0": 0})
                units.append({"kind": "p", "h": h, "t": 1, "c0": 512})
            for pi, u in enumerate(units):
                u["ui"] = pi
                u["p"] = pi % 2
            head_slots = {}
            NU = len(units)
            for i in range(NU + 4):
                uP = units[i] if i < NU else None
                uE = units[i - 1] if 0 <= i - 1 < NU else None
                uR1 = units[i - 2] if 0 <= i - 2 < NU else None
                uR2 = units[i - 3] if 0 <= i - 3 < NU else None
                sP = uP["h"] if (seg == 0 and uP is not None and uP["t"] == 1) else None
                sR = uR2["h"] if (seg == 0 and uR2 is not None and uR2["t"] == 1) else None
                late_p = (seg == 1)
                if i == 0 and late_p:
                    head_slots[0] = (w_get(), w_get(), w_get())
                    uP["slots"] = head_slots[0]
                    P_pe(uP)
                if uR1 is not None:
                    R_early(uR1)
                if sR is not None:
                    SR_stage({"h": sR})
                if uP is not None and not late_p:
                    if uP["h"] not in head_slots:
                        head_slots[uP["h"]] = (w_get(), w_get(), w_get())
                    uP["slots"] = head_slots[uP["h"]]
                    P_pe(uP)
                if uE is not None:
                    E1(uE)
                if uP is not None:
                    P_act(uP)
                if uR1 is not None:
                    R_mid(uR1)
                if uE is not None:
                    E_act(uE)
                if uR2 is not None:
                    R_chain(uR2)
                if uP is not None:
                    P_dve(uP)
                if uE is not None:
                    E2(uE)
                if sP is not None:
                    su = {"h": sP, "slots": head_slots[sP]}
                    SP_stage(su)
                    SP_rest(su)
                if uP is not None and uP["t"] == 1:
                    for sl in head_slots[uP["h"]]:
                        w_release(sl)
                if uR2 is not None:
                    R_norm(uR2)
                if sR is not None:
                    SR_main({"h": sR})
                if late_p and i + 1 < NU:
                    uN = units[i + 1]
                    if uN["h"] not in head_slots:
                        head_slots[uN["h"]] = (w_get(), w_get(), w_get())
                    uN["slots"] = head_slots[uN["h"]]
                    P_pe(uN)
            for m in range(8):
                sl = w_get()
                for (t, c0, n) in seg_tiles(seg):
                    b = psum_next()
                    for h in range(8):
                        mm(ps[:, b, 0:n], ring[:, sl, h, :], oT[:, h, c0:c0 + n], h == 0, h == 7,
                           [R_slot[sl], R_o[(h, t)]], [R_ps[b]])
                    tt("dve", xT[:, m, c0:c0 + n], ps[:, b, 0:n], xT[:, m, c0:c0 + n], ALU.add,
                       [R_ps[b], R_x[(m, t)]], [R_x[(m, t)]])
                w_release(sl)

        def mlp_phase(seg, l):
            rmsnorm(seg, C_MN0 if l == 0 else C_MN1, uT, R_u)
            ri = 0
            for fb in range(2):
                for m in range(16):
                    sl = w_get()
                    for (t, c0, n) in seg_tiles(seg):
                        b = psum_next()
                        for k in range(8):
                            mm(ps[:, b, 0:n], ring[:, sl, k, :], uT[:, k, c0:c0 + n], k == 0, k == 7,
                               [R_slot[sl], R_u[(k, t)]], [R_ps[b]])
                        r = gt[:, ri % 2, 0:n]
                        Rr = R_gt[ri % 2]
                        ri += 1
                        act(r, ps[:, b, 0:n], AF.Relu, [R_ps[b]], [Rr])
                        tt("pool", hT[:, m, c0:c0 + n], r, r, ALU.mult, [Rr], [R_h[(m, t)]])
                    w_release(sl)
                for m in range(8):
                    sl0 = w_get()
                    sl1 = w_get()
                    for (t, c0, n) in seg_tiles(seg):
                        b = psum_next()
                        for k in range(16):
                            sl = sl0 if k < 8 else sl1
                            mm(ps[:, b, 0:n], ring[:, sl, k % 8, :], hT[:, k, c0:c0 + n], k == 0, k == 15,
                               [R_slot[sl], R_h[(k, t)]], [R_ps[b]])
                        tt("dve", xT[:, m, c0:c0 + n], ps[:, b, 0:n], xT[:, m, c0:c0 + n], ALU.add,
                           [R_ps[b], R_x[(m, t)]], [R_x[(m, t)]])
                    w_release(sl0)
                    w_release(sl1)

        def pool_phase(seg):
            for (t, c0, n) in seg_tiles(seg):
                b = psum_next()
                for q4 in range(2):
                    k0 = 4 * q4
                    g0 = 2 + 2 * q4
                    sqb = gt[:, g0:g0 + 2, :].rearrange("p a b -> p (a b)").bitcast(BF16).rearrange("p (k c) -> p k c", c=512)
                    Rsq = [R_gt[g0], R_gt[g0 + 1]]
                    act(sqb[:, :, 0:n], xT[:, k0:k0 + 4, c0:c0 + n], AF.Square, [R_x[(k0 + i, t)] for i in range(4)], Rsq)
                    for i in range(4):
                        k = k0 + i
                        mm(ps[:, b, 0:n], onesb[:], sqb[:, i, 0:n], k == 0, k == 7, [R_const] + Rsq, [R_ps[b]])
                gi = (12, 13, 6)[t]
                rstd = gt[:, gi, 0:n]
                act(rstd, ps[:, b, 0:n], AF.Ln, [R_ps[b]], [R_gt[gi]], bias=EPSB[:], scale=1.0 / D)
                act(rstd, rstd, AF.Exp, [R_gt[gi]], [R_gt[gi]], scale=-0.5)
                for k in range(8):
                    if t < 2:
                        dst = uT[:, k, c0:c0 + n]
                        Rd = R_u[(k, t)]
                    else:
                        dst = ups[:, k, :]
                        Rd = R_ups
                    stt(dst, xT[:, k, c0:c0 + n], pcol[:, C_PN + k: C_PN + k + 1], rstd, ALU.mult, ALU.mult,
                        [R_x[(k, t)], R_gt[gi], R_pcol], [Rd])
                if seg == 1 and t == 1:
                    for k in range(8):
                        stt(ups[:, k, 0:15], xT[:, k, SEGP - 15:SEGP], pcol[:, C_PN + k: C_PN + k + 1], rstd[:, 512 - 15:512],
                            ALU.mult, ALU.mult, [R_x[(k, 1)], R_gt[gi], R_pcol], [R_ups])
                    b = psum_next()
                    b2 = psum_next()
                    for k in range(8):
                        bb = b if k < 4 else b2
                        tr(ps[0:15, bb, (k % 4) * 128:(k % 4 + 1) * 128], ups[:, k, 0:15], ident[:],
                           [R_ups, R_const], [R_ps[bb]])
                    cp("act", stage[0:15, 0, 0:512], ps[0:15, b, :], [R_ps[b]], [R_stage[0]])
                    cp("dve", stage[0:15, 0, 512:1024], ps[0:15, b2, :], [R_ps[b2]], [R_stage[0]])
                    dma("sp", pp, stage[0:15, 0, :], [R_stage[0]], [], "stg0", is_out=True)
            if seg == 0:
                pool_samples_load()
            slots = [w_get() for _ in range(8)]
            Y1 = Sbig[:].rearrange("p a n v -> p (a n v)").bitcast(BF16).rearrange("p (j e) -> p j e", e=D)
            fence([r for rr in R_Sbig for r in rr], R_utok)
            for j in range(8):
                t = j // 4
                b = psum_next()
                b2 = psum_next()
                for m in range(8):
                    bb = b if m < 4 else b2
                    g = m // 2
                    for kk in range(2):
                        mm(ps[:, bb, (m % 4) * 128:(m % 4 + 1) * 128], uT[:, 2 * g + kk, j * 128:(j + 1) * 128],
                           ring[:, slots[m], kk, :], kk == 0, kk == 1, [R_u[(2 * g + kk, t)], R_slot[slots[m]]], [R_ps[bb]])
                cp("act", Y1[:, j, 0:512], ps[:, b, :], [R_ps[b]], [R_utok[j]])
                cp("dve", Y1[:, j, 512:1024], ps[:, b2, :], [R_ps[b2]], [R_utok[j]])
            for m in (4, 5, 6, 7, 0, 1, 2, 3):
                wi = m // 2
                msl = slice(m * 128, (m + 1) * 128)
                for t in range(2):
                    bz = psum_next()
                    for c in range(4):
                        j = t * 4 + c
                        cs_ = slice(c * 128, (c + 1) * 128)
                        first = True
                        if j > 0:
                            mm(ps[:, bz, cs_], Y1[:, j - 1, msl], bandc[:, 4 + wi, :], True, False,
                               [R_utok[j - 1], R_band], [R_ps[bz]])
                            first = False
                        elif seg == 1:
                            mm(ps[:, bz, cs_], utokprev[:, msl], bandc[:, 4 + wi, :], True, False,
                               [R_utokprev, R_band], [R_ps[bz]])
                            first = False
                        bm = bandc[:, 8 + wi, :] if (seg == 0 and j == 0) else bandc[:, wi, :]
                        mm(ps[:, bz, cs_], Y1[:, j, msl], bm, first, True, [R_utok[j], R_band], [R_ps[bz]])
                    c0 = t * 512
                    stt(xT[:, m, c0:c0 + 512], ps[:, bz, :], pcol[:, C_PS + m: C_PS + m + 1], xT[:, m, c0:c0 + 512],
                        ALU.mult, ALU.add, [R_ps[bz], R_pcol, R_x[(m, t)]], [R_x[(m, t)]])
            if seg == 0:
                cp("pool", utokprev[:], Y1[:, 7, :], [R_utok[7]], [R_utokprev])
                pool_samples_compute()
                t, c0, n = 2, 1024, NS
                for m in range(8):
                    sl = slots[m]
                    g = m // 2
                    b = psum_next()
                    for kk in range(2):
                        mm(ps[:, b, 0:n], ring[:, sl, kk, :], uT[:, 2 * g + kk, c0:c0 + n], kk == 0, kk == 1,
                           [R_slot[sl], R_u[(2 * g + kk, t)]], [R_ps[b]])
                    stt(xT[:, m, c0:c0 + n], ps[:, b, 0:n], pcol[:, C_PS + m: C_PS + m + 1], xT[:, m, c0:c0 + n],
                        ALU.mult, ALU.add, [R_ps[b], R_pcol, R_x[(m, t)]], [R_x[(m, t)]])
            for sl in slots:
                w_release(sl)

        R_ups = Res("ups")
        R_reds = Res("reds")

        def histT_of(half):
            return gt[:, 8 + 2 * half: 10 + 2 * half, :].rearrange("p a b -> p (a b)")[:, 0:8 * 120].rearrange("p (k r) -> p k r", r=120)

        def pool_samples_load():
            for half in range(2):
                Rh = [R_gt[8 + 2 * half], R_gt[9 + 2 * half]]
                histT = histT_of(half)
                dma("sp", stage[0:120, half, :], spl[half * 8: half * 8 + 8].rearrange("n j e -> (n j) e"), [],
                    [R_stage[half]], "stg%d" % half)
                dma("sp", pps[half * 8: half * 8 + 8, 0:14, :], spl[half * 8: half * 8 + 8, 1:15, :], [], [],
                    "ppsc%d" % half, is_out=True)
                b = psum_next()
                b2 = psum_next()
                for k in range(8):
                    bb = b if k < 4 else b2
                    tr(ps[:, bb, (k % 4) * 128:(k % 4) * 128 + 120], stage[0:120, half, k * 128:(k + 1) * 128],
                       ident[0:120, 0:120], [R_stage[half], R_const], [R_ps[bb]])
                cp("act", histT[:, 0:4, :], ps[:, b, :].rearrange("p (k c) -> p k c", c=128)[:, :, 0:120], [R_ps[b]], Rh)
                cp("act", histT[:, 4:8, :], ps[:, b2, :].rearrange("p (k c) -> p k c", c=128)[:, :, 0:120], [R_ps[b2]], Rh)

        def pool_samples_compute():
            for half in range(2):
                Rh = [R_gt[8 + 2 * half], R_gt[9 + 2 * half]]
                histT = histT_of(half)
                for kg in range(4):
                    w = WINDOWS[kg]
                    k0 = 2 * kg
                    hv = histT[:, k0:k0 + 2, :].rearrange("p k (n j) -> p k n j", j=15)
                    rd = reds[:, 0:16].rearrange("p (k n) -> p k n", n=8)
                    S.add("dve", lambda e, hv=hv, w=w, rd=rd: e.tensor_reduce(out=rd, in_=hv[:, :, :, 15 - (w - 1):15],
                                                                              axis=AX.X, op=ALU.add), Rh, [R_reds])
                    us = ups[:, k0:k0 + 2, half * 8: half * 8 + 8]
                    tt("dve", rd, rd, us, ALU.add, [R_reds, R_ups], [R_reds])
                    stt(uT[:, k0:k0 + 2, 1024 + half * 8: 1024 + half * 8 + 8], rd, 1.0 / w, us, ALU.mult, ALU.subtract,
                        [R_reds, R_ups], [R_u[(k0, 2)], R_u[(k0 + 1, 2)]])
            b = psum_next()
            b2 = psum_next()
            for k in range(8):
                bb = b if k < 4 else b2
                tr(ps[0:NS, bb, (k % 4) * 128:(k % 4 + 1) * 128], ups[:, k, :], ident[:], [R_ups, R_const], [R_ps[bb]])
            cp("act", stage[0:NS, 0, 0:512], ps[0:NS, b, :], [R_ps[b]], [R_stage[0]])
            cp("act", stage[0:NS, 0, 512:1024], ps[0:NS, b2, :], [R_ps[b2]], [R_stage[0]])
            dma("sp", pps[:, 14, :], stage[0:NS, 0, :], [R_stage[0]], [], "stg0", is_out=True)

        def y_chunks(seg):
            chunks = [(j, yp[seg * SEGP + j * 128: seg * SEGP + (j + 1) * 128, :], 128) for j in range(8)]
            if seg == 0:
                chunks.append((8, ys, NS))
            return chunks

        def final_out_chunk(seg, ci):
            (j, dst, rows) = y_chunks(seg)[ci]
            sg = ci % 2
            t = j // 4 if j < 8 else 2
            c0 = j * 128
            for half in range(2):
                b = psum_next()
                for kk in range(4):
                    k = half * 4 + kk
                    tr(ps[0:rows, b, kk * 128:(kk + 1) * 128], yT[:, k, c0:c0 + rows], ident[:],
                       [R_y[(k, t)], R_const], [R_ps[b]])
                eng = "act" if half == 0 else "dve"
                cp(eng, stage[0:rows, sg, half * 512:(half + 1) * 512], ps[0:rows, b, :], [R_ps[b]], [R_stage[sg]])
            dma("sp", dst, stage[0:rows, sg, :], [R_stage[sg]], [], "stg%d" % sg, is_out=True)

        Rv_all = list(R_v.values()) + list(R_o.values())
        Rh_all = list(R_h.values())
        Rup_all = list(R_up.values()) + R_uph
        Ry_all = list(R_y.values())
        R_prows = [Res("prow%d" % i) for i in range(9)]
        plist = [hgrn_norm, mlp_norm[0:8, :], mlp_norm[8:16, :], pool_norm, pool_scale, final_norm, hgrn_onorm,
                 hgrn_lb[0:8, :], hgrn_lb[8:16, :]]
        for i, p in enumerate(plist):
            dma("act", prow[8 * i:8 * i + 8, :], p, [], [R_prows[i]], "prow%d" % i)
        xl["bufs"] = [(xstage, 0, R_xs[0], "xs0"), (xstage, 1, R_xs[1], "xs1"), (xstage, 2, R_xs[2], "xs2"),
                      (stage, 0, R_stage[0], "stg0"), (stage, 1, R_stage[1], "stg1")]
        for ci in range(5):
            issue_load(0, ci)
        for ci in range(len(x_chunks(0))):
            load_chunk(0, ci)
            issue_load(0, ci + 5)
        xl["bufs"] = xl["bufs"][0:3]
        xl["n"] = 0
        b0 = psum_next()
        tr(ps[:, b0, 0:72], prow[0:72, :], ident[0:72, 0:72], R_prows + [R_const], [R_ps[b0]])
        cp("dve", pcol[:], ps[:, b0, 0:72], [R_ps[b0]], [R_pcol])
        C_HN, C_MN0, C_MN1, C_PN, C_PS, C_FN, C_ON, C_LB0, C_LB1 = [8 * i for i in range(9)]
        tt("dve", lbt[:, 0, :], pcol[:, C_LB1:C_LB1 + 8], pcol[:, C_LB0:C_LB0 + 8], ALU.subtract, [R_pcol], [R_misc])
        act(lbt[:, 1, :], lbt[:, 0, :], AF.Exp, [R_misc], [R_misc])
        ts("dve", lbt[:, 2, :], lbt[:, 1, :], 1.0, ALU.add, [R_misc], [R_misc])
        S.add("dve", lambda e: e.reciprocal(out=lbt[:, 3, :], in_=lbt[:, 2, :]), [R_misc], [R_misc])
        act(lbt[:, 4, :], lbt[:, 2, :], AF.Ln, [R_misc], [R_misc])
        tt("dve", lbt[:, 4, :], lbt[:, 0, :], lbt[:, 4, :], ALU.subtract, [R_misc], [R_lb, R_misc])
        LB = lambda h: lbt[:, 3, h:h + 1]
        LN1MLB = lambda h: lbt[:, 4, h:h + 1]

        Vt, GE0, T1, LT, BD, CSW = [gt[:, i, 0:128] for i in range(6)]
        Rb_ = [R_gt[i] for i in range(6)]
        S.add("pool", lambda e: e.iota(bandi[:], pattern=[[1, 128]], base=0, channel_multiplier=-1), [], [R_misc])
        cp("dve", Vt, bandi[:], [R_misc], [Rb_[0]])
        S.add("pool", lambda e: e.iota(bandi[:], pattern=[[1, 128]], base=1, channel_multiplier=0), [Rb_[0]], [R_misc])
        cp("dve", T1, bandi[:], [R_misc], [Rb_[2]])
        ts("dve", GE0, Vt, 0.0, ALU.is_ge, [Rb_[0]], [Rb_[1]])
        for wi, w in enumerate(WINDOWS):
            ts("dve", LT, Vt, w - 0.5, ALU.is_lt, [Rb_[0]], [Rb_[3]])
            tt("dve", BD, LT, GE0, ALU.mult, [Rb_[3], Rb_[1]], [Rb_[4]])
            stt(bandc[:, wi, :], BD, 1.0 / w, ident[:], ALU.mult, ALU.subtract, [Rb_[4], R_const], [R_band])
            ts("dve", bandc[:, 4 + wi, :], Vt, w - 128.5, ALU.is_lt, [Rb_[0]], [R_band], s2=1.0 / w, op1=ALU.mult)
            ts("dve", CSW, T1, float(w), ALU.min, [Rb_[2]], [Rb_[5]])
            S.add("dve", lambda e: e.reciprocal(out=CSW, in_=CSW), [Rb_[5]], [Rb_[5]])
            tt("dve", LT, BD, CSW, ALU.mult, [Rb_[4], Rb_[5]], [Rb_[3]])
            tt("dve", bandc[:, 8 + wi, :], LT, ident[:], ALU.subtract, [Rb_[3], R_const], [R_band])
        S.add("dve", lambda e: e.memset(utokprev[:], 0.0), [], [R_utokprev])
        for seg in range(2):
            hgrn_phase(seg)
            fence(Rv_all, Rh_all)
            mlp_phase(seg, 0)
            fence(Rh_all, Rup_all)
            pool_phase(seg)
            fence(Rup_all, Rh_all)
            mlp_phase(seg, 1)
            fence(Rh_all, Ry_all)
            if seg == 0:
                for ci in range(3):
                    issue_load(1, ci)
            rmsnorm(seg, C_FN, yT, R_y)
            nin = len(x_chunks(1)) if seg == 0 else 0
            for ci in range(len(y_chunks(seg))):
                final_out_chunk(seg, ci)
                if ci < nin:
                    load_chunk(1, ci)
                    issue_load(1, ci + 3)
            fence(Ry_all, Rv_all)
        dma("sp", hp.rearrange("h d v -> d h v"), Sst[:], R_S, [], "hpout", is_out=True)
        fin = S.add("sp", None, [], [])
        fin.deps = set(out_dma_ops)
        assert wstate["next_get"] == len(pieces), (wstate, len(pieces))
        S.emit(nc, stack, block)
    return nc


def kernel(x_prompt, x_sample, state_hgrn, state_pool, hgrn_norm, hgrn_w_in, hgrn_lb, hgrn_onorm,
           hgrn_w_out, pool_norm, pool_w, pool_scale, mlp_norm, mlp_up, mlp_down, final_norm):
    f = lambda a: np.ascontiguousarray(np.asarray(a, dtype=np.float32))
    shared = {
        "hgrn_norm": f(hgrn_norm).reshape(8, 128),
        "w_in": f(hgrn_w_in).reshape(D, 4 * D),
        "hgrn_lb": f(hgrn_lb).reshape(16, 128),
        "hgrn_onorm": f(hgrn_onorm).reshape(8, 128),
        "w_out": f(hgrn_w_out).reshape(D, D),
        "pool_norm": f(pool_norm).reshape(8, 128),
        "pool_w": f(pool_w).reshape(4, 256, 256),
        "pool_scale": f(pool_scale).reshape(8, 128),
        "mlp_norm": f(mlp_norm).reshape(16, 128),
        "mlp_up": f(mlp_up),
        "mlp_down": f(mlp_down),
        "final_norm": f(final_norm).reshape(8, 128),
    }
    xpr = f(x_prompt)
    xsm = f(x_sample)
    shg = f(state_hgrn)
    spo = f(state_pool)
    in_maps = []
    for c in range(NCORES):
        m = dict(shared)
        m["xp"] = xpr[c]
        m["xs"] = np.ascontiguousarray(xsm[NS * c: NS * (c + 1), 0, :])
        m["sh"] = np.ascontiguousarray(shg[0, NS * c: NS * (c + 1)])
        m["spl"] = np.ascontiguousarray(spo[0, NS * c: NS * (c + 1)])
        in_maps.append(m)
    nc = build()
    res = run_bass_kernel_spmd(nc, in_maps, core_ids=list(range(NCORES)))
    rs = res.results
    y_prompt = np.stack([rs[c]["yp"] for c in range(NCORES)], axis=0).astype(np.float32)
    y_sample = np.concatenate([rs[c]["ys"] for c in range(NCORES)], axis=0).reshape(NCORES * NS, 1, D).astype(np.float32)
    hgrn_p = np.stack([rs[c]["hp"] for c in range(NCORES)], axis=0)[None].astype(np.float32)
    hgrn_s = np.concatenate([rs[c]["hs"] for c in range(NCORES)], axis=0)[None].astype(np.float32)
    pool_p = np.stack([rs[c]["pp"] for c in range(NCORES)], axis=0)[None].astype(np.float32)
    pool_s = np.concatenate([rs[c]["pps"] for c in range(NCORES)], axis=0)[None].astype(np.float32)
    return (y_prompt, y_sample, hgrn_p, hgrn_s, pool_p, pool_s)


if __name__ == "__main__":
    import time
    t0 = time.time()
    nc = build()
    print("built in", time.time() - t0)
```

```python
import math
from contextlib import ExitStack
import numpy as np
import concourse.bass as bass
import concourse.mybir as mybir
from concourse.bass_utils import run_bass_kernel_spmd

F32 = mybir.dt.float32
BF16 = mybir.dt.bfloat16
I32 = mybir.dt.int32
AF = mybir.ActivationFunctionType
ALU = mybir.AluOpType
AX = mybir.AxisListType

NCORES = 8
D = 1024
L = 2048
NS = 16
SEGP = 1024
NCOL = SEGP + NS
EPS = 1e-6
RING = 8
NG = 16
LN_QSCALE = math.log(128 ** -0.5)
WINDOWS = (2, 4, 8, 16)


class Res:
    __slots__ = ("name", "w", "r", "const", "excl")

    def __init__(self, name, const=False, excl=False):
        self.name = name
        self.w = None
        self.r = {}
        self.const = const
        self.excl = excl


class Op:
    __slots__ = ("eng", "fn", "deps", "chan", "ticket", "semval", "needed", "uid")


ENGS = ("pe", "act", "dve", "pool", "sp")


class Sched:
    def __init__(self):
        self.q = {e: [] for e in ENGS}
        self.ops = []

    def add(self, eng, fn, reads=(), writes=(), chan=None):
        op = Op()
        op.eng = eng
        op.fn = fn
        op.chan = chan
        op.ticket = None
        op.semval = None
        op.needed = False
        op.uid = len(self.ops)
        deps = set()
        for r in reads:
            if r.w is not None:
                deps.add(r.w)
            if r.excl:
                for o in r.r.values():
                    if o.eng != eng:
                        deps.add(o)
        for r in writes:
            if r.w is not None:
                deps.add(r.w)
            for o in r.r.values():
                deps.add(o)
        key = eng if chan is None else ("dma", op.uid)
        for r in reads:
            if not r.const:
                r.r[key] = op
        for r in writes:
            r.w = op
            r.r = {}
        deps.discard(op)
        op.deps = deps
        self.q[eng].append(op)
        self.ops.append(op)
        return op

    def emit(self, nc, stack, block):
        for op in self.ops:
            for d in op.deps:
                if d.chan is None and not (d.eng == "pe" and op.eng == "pe"):
                    d.needed = True
        for e in ENGS:
            cnt = 0
            for op in self.q[e]:
                if op.chan is None and op.needed:
                    cnt += 1
                    op.ticket = cnt
        chan_cnt = {}
        chan_sem = {}
        for op in self.ops:
            if op.chan is not None:
                chan_cnt[op.chan] = chan_cnt.get(op.chan, 0) + 16
                op.semval = chan_cnt[op.chan]
                if op.chan not in chan_sem:
                    chan_sem[op.chan] = stack.enter_context(nc.semaphore("c_" + op.chan))
        esem = {e: stack.enter_context(nc.semaphore("e_" + e)) for e in ENGS}

        def run(e, handle):
            seen = {}
            for op in self.q[e]:
                waits = {}
                for d in op.deps:
                    if d.chan is not None:
                        k = ("c", d.chan)
                        v = d.semval
                    else:
                        if d.eng == "pe" and e == "pe":
                            continue
                        k = ("e", d.eng)
                        v = d.ticket
                    if v > waits.get(k, 0):
                        waits[k] = v
                for k, v in waits.items():
                    if seen.get(k, 0) >= v:
                        continue
                    seen[k] = v
                    sem = chan_sem[k[1]] if k[0] == "c" else esem[k[1]]
                    handle.wait_ge(sem, v)
                if op.fn is None:
                    continue
                ins = op.fn(handle)
                if op.chan is not None:
                    ins.then_inc(chan_sem[op.chan], 16)
                elif op.needed:
                    ins.then_inc(esem[e], 1)

        @block.tensor
        def _(h):
            run("pe", h)

        @block.scalar
        def _(h):
            run("act", h)

        @block.vector
        def _(h):
            run("dve", h)

        @block.gpsimd
        def _(h):
            run("pool", h)

        @block.sync
        def _(h):
            run("sp", h)


def build():
    nc = bass.Bass("TRN2", target_bir_lowering=False)
    S = Sched()

    def din(name, shape):
        return nc.dram_tensor(name, list(shape), F32, kind="ExternalInput").ap()

    def dout(name, shape):
        return nc.dram_tensor(name, list(shape), F32, kind="ExternalOutput").ap()

    xp = din("xp", [L, D])
    xs = din("xs", [NS, D])
    sh = din("sh", [NS, 8, 128, 128])
    spl = din("spl", [NS, 15, D])
    hgrn_norm = din("hgrn_norm", [8, 128])
    w_in = din("w_in", [D, 4 * D])
    hgrn_lb = din("hgrn_lb", [16, 128])
    hgrn_onorm = din("hgrn_onorm", [8, 128])
    w_out = din("w_out", [D, D])
    pool_norm = din("pool_norm", [8, 128])
    pool_w = din("pool_w", [4, 256, 256])
    pool_scale = din("pool_scale", [8, 128])
    mlp_norm = din("mlp_norm", [16, 128])
    mlp_up = din("mlp_up", [2, D, 4 * D])
    mlp_down = din("mlp_down", [2, 4 * D, D])
    final_norm = din("final_norm", [8, 128])

    yp = dout("yp", [L, D])
    ys = dout("ys", [NS, D])
    hp = dout("hp", [8, 128, 128])
    hs = dout("hs", [NS, 8, 128, 128])
    pp = dout("pp", [15, D])
    pps = dout("pps", [NS, 15, D])

    stack = ExitStack()
    with stack:
        def sb(name, shape, dt):
            return stack.enter_context(nc.sbuf_tensor(name, list(shape), dt))

        xT = sb("xT", [128, 8, NCOL], F32)
        uT = sb("uT", [128, 8, NCOL], BF16)
        BIG = sb("BIG", [128, 17536], BF16)
        ring = sb("ring", [128, RING, 8, 128], BF16)
        stage = sb("stage", [128, 2, D], F32)
        xstage = sb("xstage", [128, 3, D], F32)
        gt = sb("gt", [128, NG, 512], F32)
        kTt = sb("kTt", [128, 2, 512], BF16)
        kLt = sb("kLt", [128, 2, 512], BF16)
        qTt = sb("qTt", [128, 2, 512], BF16)
        qSt = sb("qSt", [128, 3, 512], BF16)
        cst = sb("cst", [128, 3, 12], F32)
        dft = sb("dft", [128, 4, 12], F32)
        gmt = sb("gmt", [128, 4, 4], F32)
        am = sb("am", [128, 2, 512], BF16)
        ktok = sb("ktok", [128, 1, 512], BF16)
        Sbt = sb("Sbt", [128, 9, 128], BF16)
        Sbig = sb("Sbig", [128, 2, NS, 128], F32)
        kmask = sb("kmask", [16, NS, 128], BF16)
        sqn = sb("sqn", [128, 2, 512], BF16)
        Sst = sb("Sst", [128, 8, 128], F32)
        Sb16 = sb("Sb16", [128, 2, 4, 128], BF16)
        smp = sb("smp", [128, 16, NS], F32)
        qS = sb("qS", [128, 2, NS], BF16)
        histsave = sb("histsave", [128, 8, 15], F32)
        ident = sb("ident", [128, 128], F32)
        identb = sb("identb", [128, 128], BF16)
        onesf = sb("onesf", [128, 512], BF16)
        onesb = sb("onesb", [128, 128], BF16)
        mask4 = sb("mask4", [128, 4, 128], BF16)
        ktoks = sb("ktoks", [16, 128], F32)
        bandc = sb("bandc", [128, 12, 128], BF16)
        utokprev = sb("utokprev", [128, D], BF16)
        bandi = sb("bandi", [128, 128], I32)
        prow = sb("prow", [72, 128], F32)
        pcol = sb("pcol", [128, 72], F32)
        lbt = sb("lbt", [128, 5, 8], F32)
        invc = sb("invc", [128, 4, 16], F32)
        cnti = sb("cnti", [128, 16], I32)
        cntf = sb("cntf", [128, 16], F32)
        fsc = sb("fsc", [128, 1], F32)
        EPSB = sb("epsb", [128, 1], F32)
        ONEB = sb("oneb", [128, 1], F32)
        LNQ = sb("lnq", [128, 1], F32)
        ups = sb("ups", [128, 8, NS], F32)
        reds = sb("reds", [128, NS], F32)
        ps = stack.enter_context(nc.psum_tensor("ps", [128, 8, 512], F32))
        block = stack.enter_context(nc.Block())

        vtok = BIG[:, 0:9216].rearrange("p (j e) -> p j e", e=1024)
        oT = BIG[:, 9216:9216 + 8320].rearrange("p (h c) -> p h c", c=NCOL)
        hT = BIG[:, 0:16640].rearrange("p (m c) -> p m c", c=NCOL)
        bigf = BIG[:, 0:16640].bitcast(F32).rearrange("p (k c) -> p k c", c=NCOL)
        upT = bigf
        yT = bigf

        R_ps = [Res("ps%d" % b, excl=True) for b in range(8)]
        R_slot = [Res("slot%d" % s) for s in range(RING)]
        R_stage = [Res("stage%d" % i) for i in range(2)]
        R_xs = [Res("xstage%d" % i) for i in range(3)]
        R_gt = [Res("gt%d" % i) for i in range(NG)]
        R_x = {(k, t): Res("x%d_%d" % (k, t)) for k in range(8) for t in range(3)}
        R_u = {(k, t): Res("u%d_%d" % (k, t)) for k in range(8) for t in range(3)}
        R_v = {(g, j): Res("v%d_%d" % (g, j)) for g in range(2) for j in range(9)}
        R_o = {(h, t): Res("o%d_%d" % (h, t)) for h in range(8) for t in range(3)}
        R_h = {(m, t): Res("h%d_%d" % (m, t)) for m in range(16) for t in range(3)}
        R_up = {(k, t): Res("up%d_%d" % (k, t)) for k in range(8) for t in range(3)}
        R_uph = [Res("uph%d" % k) for k in range(8)]
        R_y = {(k, t): Res("y%d_%d" % (k, t)) for k in range(8) for t in range(3)}
        R_S = [Res("S%d" % h) for h in range(8)]
        R_kT = [Res("kT%d" % i) for i in range(4)]
        R_kL = [Res("kL%d" % i) for i in range(4)]
        R_sqpair = [Res("sqpair0"), Res("sqpair1")]
        R_qT = [Res("qT%d" % i) for i in range(4)]
        R_qSt = [Res("qSt%d" % i) for i in range(3)]
        R_lg = [Res("lg%d" % i) for i in range(4)]
        R_cs = [Res("cs%d" % i) for i in range(4)]
        R_df = [Res("df%d" % i) for i in range(4)]
        R_gm = [Res("gm%d" % i) for i in range(4)]
        R_am = [Res("am%d" % i) for i in range(2)]
        R_ktok = [Res("ktok%d" % i) for i in range(2)]
        R_Sb = [Res("Sb%d" % i) for i in range(9)]
        R_sqn = [Res("sqn%d" % i) for i in range(2)]
        R_Sbig = [[Res("Sbig%d_%d" % (i, g)) for g in range(4)] for i in range(2)]
        R_Sb16 = [Res("Sb16%d" % g) for g in range(4)]
        R_kmask = Res("kmask")
        R_utok = [Res("utok%d" % j) for j in range(8)]
        R_utokprev = Res("utokprev")
        R_band = Res("band", const=True)
        R_ktoks = Res("ktoks")
        R_smp = [Res("smp%d" % i) for i in range(16)]
        R_qS = [Res("qS%d" % i) for i in range(2)]
        R_hsave = Res("histsave")
        R_const = Res("const", const=True)
        R_prow = Res("prow")
        R_pcol = Res("pcol", const=True)
        R_lb = Res("lb", const=True)
        R_misc = Res("misc")
        R_ones = Res("ones", const=True)
        R_fsc = Res("fsc")
        out_dma_ops = []

        def act(out, in_, func, R, W, bias=None, scale=None):
            kw = {}
            if bias is not None:
                kw["bias"] = bias
            if scale is not None:
                kw["scale"] = scale
            return S.add("act", lambda e: e.activation(out=out, in_=in_, func=func, **kw), R, W)

        def tt(eng, out, in0, in1, op, R, W):
            return S.add(eng, lambda e: e.tensor_tensor(out=out, in0=in0, in1=in1, op=op), R, W)

        def ts(eng, out, in0, s1, op0, R, W, s2=None, op1=None):
            if op1 is None:
                return S.add(eng, lambda e: e.tensor_scalar(out=out, in0=in0, scalar1=s1, scalar2=None, op0=op0), R, W)
            return S.add(eng, lambda e: e.tensor_scalar(out=out, in0=in0, scalar1=s1, scalar2=s2, op0=op0, op1=op1), R, W)

        def stt(out, in0, scalar, in1, op0, op1, R, W):
            return S.add("dve", lambda e: e.scalar_tensor_tensor(out=out, in0=in0, scalar=scalar, in1=in1, op0=op0, op1=op1), R, W)

        def cp(eng, out, in_, R, W):
            if eng == "act":
                return S.add("act", lambda e: e.copy(out=out, in_=in_), R, W)
            return S.add(eng, lambda e: e.tensor_copy(out=out, in_=in_), R, W)

        def mm(out, lhsT, rhs, start, stop, R, W):
            return S.add("pe", lambda e: e.matmul(out, lhsT=lhsT, rhs=rhs, start=start, stop=stop, skip_group_check=True), R, W)

        def tr(out, in_, idn, R, W):
            return S.add("pe", lambda e: e.transpose(out, in_, idn), R, W)

        def dma(q, out, in_, R, W, chan, is_out=False):
            op = S.add(q, lambda e: e.dma_start(out=out, in_=in_), R, W, chan=chan)
            if is_out:
                out_dma_ops.append(op)
            return op

        psum_i = [0]
        psum_free_list = list(range(8))
        psum_last = [0] * 8

        def psum_next():
            b = psum_alloc()
            psum_release(b)
            return b

        def psum_alloc():
            assert psum_free_list, "out of PSUM banks"
            b = min(psum_free_list, key=lambda i: psum_last[i])
            psum_free_list.remove(b)
            psum_i[0] += 1
            psum_last[b] = psum_i[0]
            return b

        def psum_release(b):
            psum_i[0] += 1
            psum_last[b] = psum_i[0]
            psum_free_list.append(b)

        def fence(old, new):
            op = S.add("dve", lambda e: e.memset(fsc[:], 0.0), [], list(old) + [R_fsc])
            for r in new:
                r.w = op
                r.r = {}

        def wpiece(ap2d, nk=8):
            return (ap2d.rearrange("(kc p) e -> p kc e", p=128), nk)

        pieces = []
        for seg in range(2):
            for h in range(8):
                pieces.append(wpiece(w_in[:, 2048 + h * 128: 2048 + (h + 1) * 128]))
            for h in range(8):
                for off in (0, 1024, 3072):
                    pieces.append(wpiece(w_in[:, off + h * 128: off + (h + 1) * 128]))
            for m in range(8):
                pieces.append(wpiece(w_out[:, m * 128:(m + 1) * 128]))
            for l in range(2):
                if l == 1:
                    for m in range(8):
                        g = m // 2
                        pieces.append(wpiece(pool_w[g][:, (m % 2) * 128:(m % 2 + 1) * 128], 2))
                for fb in range(2):
                    for m in range(16):
                        c0 = (fb * 16 + m) * 128
                        pieces.append(wpiece(mlp_up[l][:, c0:c0 + 128]))
                    for m in range(8):
                        for half in range(2):
                            r0 = fb * 2048 + half * 1024
                            pieces.append(wpiece(mlp_down[l][r0:r0 + 1024, m * 128:(m + 1) * 128]))
        wstate = {"next_load": 0, "next_get": 0}

        def w_issue(slot):
            i = wstate["next_load"]
            if i >= len(pieces):
                return
            wstate["next_load"] += 1
            ap, nk = pieces[i]
            dma("pool", ring[:, slot, 0:nk, :], ap, [], [R_slot[slot]], "w%d" % slot)

        def w_get():
            i = wstate["next_get"]
            wstate["next_get"] += 1
            return i % RING

        def w_release(slot):
            w_issue(slot)

        S.add("pool", lambda e: e.memset(onesf[:], 1.0), [], [R_ones])
        S.add("pool", lambda e: e.affine_select(out=ident[:], in_=onesf[:, 0:128], pattern=[[1, 128]],
                                                 compare_op=ALU.is_equal, fill=0.0, base=0, channel_multiplier=-1),
              [R_ones], [R_const])
        S.add("pool", lambda e: e.affine_select(out=identb[:], in_=onesf[:, 0:128], pattern=[[1, 128]],
                                                 compare_op=ALU.is_equal, fill=0.0, base=0, channel_multiplier=-1),
              [R_ones], [R_const])
        S.add("pool", lambda e: e.affine_select(out=mask4[:], in_=onesf[:].rearrange("p (c j) -> p c j", j=128),
                                                 pattern=[[0, 4], [1, 128]], compare_op=ALU.is_ge, fill=0.0, base=0,
                                                 channel_multiplier=-1), [R_ones], [R_const])
        S.add("pool", lambda e: e.memset(onesb[:], 1.0), [], [R_const])
        S.add("pool", lambda e: e.iota(cnti[:], pattern=[[1, 16]], base=1, channel_multiplier=0), [], [R_misc])
        cp("dve", cntf[:], cnti[:], [R_misc], [R_misc])
        for wi, w in enumerate(WINDOWS):
            ts("dve", invc[:, wi, :], cntf[:], float(w), ALU.min, [R_misc], [R_misc])
        S.add("dve", lambda e: e.reciprocal(out=invc[:], in_=invc[:]), [R_misc], [R_misc])
        for h in range(8):
            S.add("dve", lambda e, h=h: e.memset(Sst[:, h, :], 0.0), [], [R_S[h]])
        S.add("dve", lambda e: e.memset(histsave[:], 0.0), [], [R_hsave])

        for s in range(RING):
            w_issue(s)

        def seg_tiles(seg):
            tl = [(0, 0, 512), (1, 512, 512)]
            if seg == 0:
                tl.append((2, 1024, NS))
            return tl

        def x_chunks(seg):
            chunks = [(j, xp[seg * SEGP + j * 128: seg * SEGP + (j + 1) * 128, :], 128) for j in range(8)]
            if seg == 0:
                chunks.append((8, xs, NS))
            return chunks

        xl = {"n": 0, "buf": {}, "bufs": None}

        def issue_load(seg, ci):
            chunks = x_chunks(seg)
            if ci >= len(chunks):
                return
            (j, src, rows) = chunks[ci]
            bufd = xl["bufs"][xl["n"] % len(xl["bufs"])]
            xl["n"] += 1
            xl["buf"][(seg, ci)] = bufd
            dma("sp", bufd[0][0:rows, bufd[1], :], src, [], [bufd[2]], bufd[3])

        def load_chunk(seg, ci):
            (j, src, rows) = x_chunks(seg)[ci]
            xbuf, sg, Rxb, _ = xl["buf"][(seg, ci)]
            t = j // 4 if j < 8 else 2
            c0 = j * 128
            for half in range(2):
                b = psum_next()
                for kk in range(4):
                    k = half * 4 + kk
                    tr(ps[:, b, kk * 128: kk * 128 + rows], xbuf[0:rows, sg, k * 128:(k + 1) * 128],
                       ident[0:rows, 0:rows], [Rxb, R_const], [R_ps[b]])
                eng = "act" if half == 0 else "dve"
                src_ap = ps[:, b, :].rearrange("p (k c) -> p k c", c=128)[:, :, 0:rows]
                cp(eng, xT[:, half * 4: half * 4 + 4, c0:c0 + rows], src_ap,
                   [R_ps[b]], [R_x[(half * 4 + kk, t)] for kk in range(4)])

        def rmsnorm(seg, gcol, dst, R_dst, dst_col_off=0):
            for (t, c0, n) in seg_tiles(seg):
                b = psum_next()
                for q4 in range(2):
                    k0 = 4 * q4
                    g0 = 2 + 2 * q4
                    sqb = gt[:, g0:g0 + 2, :].rearrange("p a b -> p (a b)").bitcast(BF16).rearrange("p (k c) -> p k c", c=512)
                    Rsq = [R_gt[g0], R_gt[g0 + 1]]
                    act(sqb[:, :, 0:n], xT[:, k0:k0 + 4, c0:c0 + n], AF.Square, [R_x[(k0 + i, t)] for i in range(4)], Rsq)
                    for i in range(4):
                        k = k0 + i
                        mm(ps[:, b, 0:n], onesb[:], sqb[:, i, 0:n], k == 0, k == 7, [R_const] + Rsq, [R_ps[b]])
                gi = (12, 13, 6)[t]
                rstd = gt[:, gi, 0:n]
                act(rstd, ps[:, b, 0:n], AF.Ln, [R_ps[b]], [R_gt[gi]], bias=EPSB[:], scale=1.0 / D)
                act(rstd, rstd, AF.Exp, [R_gt[gi]], [R_gt[gi]], scale=-0.5)
                for k in range(8):
                    stt(dst[:, k, dst_col_off + c0: dst_col_off + c0 + n], xT[:, k, c0:c0 + n], pcol[:, gcol + k: gcol + k + 1],
                        rstd, ALU.mult, ALU.mult, [R_x[(k, t)], R_gt[gi], R_pcol], [R_dst[(k, t)]])

        S.add("dve", lambda e: e.memset(EPSB[:], EPS), [], [R_const])
        S.add("dve", lambda e: e.memset(ONEB[:], 1.0), [], [R_const])
        S.add("dve", lambda e: e.memset(LNQ[:], LN_QSCALE), [], [R_const])

        def vproj(seg):
            chunks = list(range(8)) + ([8] if seg == 0 else [])
            for hg in range(2):
                slots = [w_get() for _ in range(4)]
                for j in chunks:
                    rows = 128 if j < 8 else NS
                    t = j // 4 if j < 8 else 2
                    c0 = j * 128
                    b = psum_next()
                    for hh in range(4):
                        sl = slots[hh]
                        for k in range(8):
                            mm(ps[0:rows, b, hh * 128:(hh + 1) * 128], uT[:, k, c0:c0 + rows], ring[:, sl, k, :],
                               k == 0, k == 7, [R_u[(k, t)], R_slot[sl]], [R_ps[b]])
                    eng = "act" if (j % 2 == 0) else "dve"
                    cp(eng, vtok[0:rows, j, hg * 512:(hg + 1) * 512], ps[0:rows, b, :], [R_ps[b]], [R_v[(hg, j)]])
                for sl in slots:
                    w_release(sl)

        GT_L1 = (0, 1)
        GT_L2 = (2, 3)
        GT_LQ = (4, 5)
        GT_QSB = (6, 7)
        GT_LG = (8, 9, 10, 15)
        GT_G, GT_EA, GT_EQ, GT_LNMS = 11, 12, 13, 14

        def norm_gate(h, t, c0, n, b_o, lg_ap, R_lg_res, si):
            act(sqn[:, si, 0:n], ps[:, b_o, 0:n], AF.Square, [R_ps[b_o]], [R_sqn[si]])
            bn = psum_next()
            mm(ps[:, bn, 0:n], onesb[:], sqn[:, si, 0:n], True, True, [R_const, R_sqn[si]], [R_ps[bn]])
            lnms = gt[:, GT_LNMS, 0:n]
            Rl = R_gt[GT_LNMS]
            act(lnms, ps[:, bn, 0:n], AF.Ln, [R_ps[bn]], [Rl], bias=EPSB[:], scale=1.0 / 128)
            stt(lnms, lnms, -0.5, lg_ap, ALU.mult, ALU.subtract, [Rl, R_lg_res], [Rl])
            act(lnms, lnms, AF.Exp, [Rl], [Rl])
            stt(oT[:, h, c0:c0 + n], ps[:, b_o, 0:n], pcol[:, C_ON + h: C_ON + h + 1], lnms, ALU.mult, ALU.mult,
                [R_ps[b_o], Rl, R_pcol], [R_o[(h, t)]])

        def proj3(u, n, c0, t):
            slq, slf, slo = u["slots"]
            Ru = [R_u[(k, t)] for k in range(8)]
            banks = []
            for sl in (slf, slq, slo):
                bk = psum_alloc()
                for k in range(8):
                    mm(ps[:, bk, 0:n], ring[:, sl, k, :], uT[:, k, c0:c0 + n], k == 0, k == 7, [Ru[k], R_slot[sl]], [R_ps[bk]])
                banks.append(bk)
            return banks

        def P_pe(u):
            u["bf"], u["bq"], u["bo"] = proj3(u, 512, u["c0"], u["t"])

        def P_act(u):
            h, p, n = u["h"], u["p"], 512
            bf, bq, bo = u["bf"], u["bq"], u["bo"]
            eA, eQ = gt[:, GT_EA, :], gt[:, GT_EQ, :]
            l1, l2, lq = gt[:, GT_L1[p], :], gt[:, GT_L2[p], :], gt[:, GT_LQ[p], :]
            lg = gt[:, GT_LG[u["ui"] % 4], :]
            act(eA, ps[:, bf, :], AF.Exp, [R_ps[bf]], [R_gt[GT_EA]], scale=-1.0)
            act(l1, eA, AF.Ln, [R_gt[GT_EA]], [R_gt[GT_L1[p]]], bias=ONEB[:])
            act(l2, eA, AF.Ln, [R_gt[GT_EA], R_lb], [R_gt[GT_L2[p]]], bias=ONEB[:], scale=LB(h))
            act(eQ, ps[:, bq, :], AF.Exp, [R_ps[bq]], [R_gt[GT_EQ]], scale=-1.0)
            act(lq, eQ, AF.Ln, [R_gt[GT_EQ]], [R_gt[GT_LQ[p]]], bias=ONEB[:])
            act(lg, ps[:, bo, :], AF.Exp, [R_ps[bo]], [R_gt[GT_LG[u["ui"] % 4]]], scale=-1.0)
            psum_release(bo)
            act(lg, lg, AF.Ln, [R_gt[GT_LG[u["ui"] % 4]]], [R_gt[GT_LG[u["ui"] % 4]]], bias=ONEB[:])

        def P_dve(u):
            p = u["p"]
            bf, bq = u["bf"], u["bq"]
            l1, l2 = gt[:, GT_L1[p], :], gt[:, GT_L2[p], :]
            tt("pool", l2, l2, l1, ALU.subtract, [R_gt[GT_L2[p]], R_gt[GT_L1[p]]], [R_gt[GT_L2[p]]])
            stt(l1, ps[:, bf, :], -1.0, l1, ALU.mult, ALU.subtract, [R_ps[bf], R_gt[GT_L1[p]]], [R_gt[GT_L1[p]]])
            psum_release(bf)

        def E1(u):
            p = u["p"]
            l1, l2, lq = gt[:, GT_L1[p], :], gt[:, GT_L2[p], :], gt[:, GT_LQ[p], :]
            R1, R2, RQ, RG = R_gt[GT_L1[p]], R_gt[GT_L2[p]], R_gt[GT_LQ[p]], R_gt[GT_G]
            G = gt[:, GT_G, :]
            S.add("dve", lambda e: e.tensor_tensor_scan(out=G, data0=onesf[:, :], data1=l2, initial=0.0,
                                                        op0=ALU.mult, op1=ALU.add), [R2, R_ones], [RG])
            G3 = G.rearrange("p (c j) -> p c j", j=128)
            Glast = G3[:, :, 127]
            gm = gmt[:, p, :]
            df = dft[:, p, :]
            Rdf = R_df[p]
            cp("dve", gm, G3[:, :, 63], [RG], [R_gm[p]])
            cp("dve", df[:, 0:1], gm[:, 0:1], [R_gm[p]], [Rdf])
            tt("dve", df[:, 1:4], gm[:, 1:4], gm[:, 0:3], ALU.subtract, [R_gm[p]], [Rdf])
            tt("dve", df[:, 4:8], Glast, gm, ALU.subtract, [RG, R_gm[p]], [Rdf])
            tt("dve", G3, G3, gm.unsqueeze(2).to_broadcast([128, 4, 128]), ALU.subtract, [RG, R_gm[p]], [RG])
            tt("pool", l1, l1, G, ALU.subtract, [R1, RG], [R1])
            tt("pool", lq, G, lq, ALU.subtract, [RG, RQ], [RQ])

        def E_act(u):
            h, p = u["h"], u["p"]
            l1, lq = gt[:, GT_L1[p], :], gt[:, GT_LQ[p], :]
            c3 = u["ui"] % 3
            act(cst[:, c3, 0:8], dft[:, p, 0:8], AF.Exp, [R_df[p]], [R_cs[c3]])
            act(kTt[:, p, :], l1, AF.Exp, [R_gt[GT_L1[p]], R_lb], [R_kT[p]], bias=LN1MLB(h))
            act(lq, lq, AF.Exp, [R_gt[GT_LQ[p]]], [R_gt[GT_LQ[p]]], bias=LNQ[:])

        def E2(u):
            p = u["p"]
            c3 = u["ui"] % 3
            bq = u["bq"]
            tt("dve", qTt[:, p, :], ps[:, bq, :], gt[:, GT_LQ[p], :], ALU.mult,
               [R_ps[bq], R_gt[GT_LQ[p]]], [R_qT[p]])
            psum_release(bq)
            m0 = cst[:, c3, 0:1]
            if u["t"] == 1:
                c3p = (u["ui"] - 1) % 3
                ts("dve", cst[:, c3, 8:10], cst[:, c3, 0:2], cst[:, c3p, 7:8], ALU.mult, [R_cs[c3], R_cs[c3p]], [R_cs[c3]])
                m0 = cst[:, c3, 8:9]
            qM3 = qSt[:, c3, :].rearrange("p (c j) -> p c j", j=128)
            qT3 = qTt[:, p, :].rearrange("p (c j) -> p c j", j=128)
            tt("pool", qM3[:, 1:4, :], qT3[:, 1:4, :], cst[:, c3, 1:4].unsqueeze(2).to_broadcast([128, 3, 128]), ALU.mult,
               [R_qT[p], R_cs[c3]], [R_qSt[c3]])
            ts("pool", qSt[:, c3, 0:128], qTt[:, p, 0:128], m0, ALU.mult, [R_qT[p], R_cs[c3]], [R_qSt[c3]], s2=0.0, op1=ALU.add)

        def R_early(u):
            h, t, p = u["h"], u["t"], u["p"]
            c3e = u["ui"] % 3
            ba = psum_alloc()
            for c in range(4):
                cs_ = slice(c * 128, (c + 1) * 128)
                mm(ps[:, ba, cs_], kTt[:, p, cs_], qTt[:, p, cs_], True, True, [R_kT[p], R_qT[p]], [R_ps[ba]])
            bt = psum_alloc()
            psb = ps[:, bt, :].bitcast(BF16)
            for c in range(4):
                cs_ = slice(c * 128, (c + 1) * 128)
                tr(psb[:, cs_], kTt[:, p, cs_], identb[:], [R_kT[p], R_const], [R_ps[bt]])
            u["ba"], u["bt"] = ba, bt

        def R_mid(u):
            h, t, c0 = u["h"], u["t"], u["c0"]
            hg = h // 4
            hc = slice(h * 128, (h + 1) * 128)
            ba, bt = u["ba"], u["bt"]
            q = u["p"]
            if t == 0:
                cp("pool", Sbt[:, 0, :], Sst[:, h, :], [R_S[h]], [R_Sb[0]])
            psb = ps[:, bt, :].bitcast(BF16)
            tt("dve", am[:, q, :], ps[:, ba, :], mask4[:].rearrange("p c j -> p (c j)"), ALU.mult,
               [R_ps[ba], R_const], [R_am[q]])
            psum_release(ba)
            cp("act", ktok[:, 0, :], psb[:, 0:512], [R_ps[bt]], [R_ktok[0]])
            psum_release(bt)
            bs = psum_alloc()
            for c in range(4):
                cs_ = slice(c * 128, (c + 1) * 128)
                j = (c0 // 128) + c
                mm(ps[:, bs, cs_], ktok[:, 0, cs_], vtok[:, j, hc], True, True, [R_ktok[0], R_v[(hg, j)]], [R_ps[bs]])
            u["bs"] = bs

        def R_chain(u):
            h, t, c0, p = u["h"], u["t"], u["c0"], u["p"]
            hg = h // 4
            hc = slice(h * 128, (h + 1) * 128)
            bs = u["bs"]
            q = u["p"]
            c3 = u["ui"] % 3
            b_o = psum_alloc()

            def mfac(c):
                if c == 0:
                    return cst[:, c3, 0:1] if t == 0 else cst[:, c3, 8:9]
                return cst[:, c3, c:c + 1]

            for c in range(4):
                cs_ = slice(c * 128, (c + 1) * 128)
                j = (c0 // 128) + c
                gc = t * 4 + c
                mm(ps[:, b_o, cs_], Sbt[:, gc, :], qSt[:, c3, cs_], True, False, [R_Sb[gc], R_qSt[c3]], [R_ps[b_o]])
                mm(ps[:, b_o, cs_], vtok[:, j, hc], am[:, q, cs_], False, True, [R_v[(hg, j)], R_am[q]], [R_ps[b_o]])
                stt(Sbt[:, gc + 1, :], Sst[:, h, :], mfac(c), ps[:, bs, cs_], ALU.mult, ALU.add,
                    [R_S[h], R_cs[c3], R_ps[bs]], [R_Sb[gc + 1]])
                stt(Sst[:, h, :], Sst[:, h, :], mfac(c), ps[:, bs, cs_], ALU.mult, ALU.add,
                    [R_S[h], R_cs[c3], R_ps[bs]], [R_S[h]])
            if t == 1:
                ts("dve", Sst[:, h, :], Sst[:, h, :], cst[:, c3, 7:8], ALU.mult, [R_S[h], R_cs[c3]], [R_S[h]])
            psum_release(bs)
            u["b_o"] = b_o

        def R_norm(u):
            li = GT_LG[u["ui"] % 4]
            norm_gate(u["h"], u["t"], u["c0"], 512, u["b_o"], gt[:, li, :], R_gt[li], 0)
            psum_release(u["b_o"])

        def SP_stage(u):
            h = u["h"]
            t, c0, n = 2, 1024, NS
            hp = h % 2
            for g4 in range(4):
                dma("sp", Sbig[:, hp, g4 * 4:(g4 + 1) * 4, :], sh[g4 * 4:(g4 + 1) * 4, h].rearrange("n d v -> d n v"), [],
                    [R_Sbig[hp][g4]], "sbig%d_%d" % (hp, g4))
            u["sbanks"] = proj3(u, n, c0, t)

        def SP_rest(u):
            h = u["h"]
            t, c0, n = 2, 1024, NS
            hp = h % 2
            bf, bq, bo = u["sbanks"]
            s0 = 8 * hp
            eA, l1, l2, bQ, fS, kS, sgq, lgs = [smp[:, s0 + i, :] for i in range(8)]
            RA, R1, R2, RQ, RfS, RkS, Rsg, Rlg = [R_smp[s0 + i] for i in range(8)]
            pf, pq, po = ps[:, bf, 0:n], ps[:, bq, 0:n], ps[:, bo, 0:n]
            act(eA, pf, AF.Exp, [R_ps[bf]], [RA], scale=-1.0)
            act(l1, eA, AF.Ln, [RA], [R1], bias=ONEB[:])
            act(l2, eA, AF.Ln, [RA, R_lb], [R2], bias=ONEB[:], scale=LB(h))
            act(bQ, pq, AF.Exp, [R_ps[bq]], [RQ], scale=-1.0)
            act(bQ, bQ, AF.Ln, [RQ], [RQ], bias=ONEB[:])
            act(lgs, po, AF.Exp, [R_ps[bo]], [Rlg], scale=-1.0)
            psum_release(bo)
            act(lgs, lgs, AF.Ln, [Rlg], [Rlg], bias=ONEB[:])
            tt("dve", l2, l2, l1, ALU.subtract, [R2, R1], [R2])
            stt(l1, pf, -1.0, l1, ALU.mult, ALU.subtract, [R_ps[bf], R1], [R1])
            psum_release(bf)
            act(fS, l2, AF.Exp, [R2], [RfS])
            act(kS, l1, AF.Exp, [R1, R_lb], [RkS], bias=LN1MLB(h))
            act(sgq, bQ, AF.Exp, [RQ], [Rsg], bias=LNQ[:], scale=-1.0)
            tt("dve", qS[:, hp, :], pq, sgq, ALU.mult, [R_ps[bq], Rsg], [R_qS[hp]])
            psum_release(bq)

        def SR_stage(u):
            h = u["h"]
            t, c0, n = 2, 1024, NS
            hp = h % 2
            hg = h // 4
            hc = slice(h * 128, (h + 1) * 128)
            s0 = 8 * hp
            fS, kS, lgs = smp[:, s0 + 4, :], smp[:, s0 + 5, :], smp[:, s0 + 7, :]
            RfS, RkS, Rlg = R_smp[s0 + 4], R_smp[s0 + 5], R_smp[s0 + 7]
            bk = psum_alloc()
            tr(ps[0:NS, bk, 0:128], kS, ident[:], [RkS, R_const], [R_ps[bk]])
            cp("dve", ktoks[:], ps[0:NS, bk, 0:128], [R_ps[bk]], [R_ktoks])
            S.add("pool", lambda e: e.affine_select(out=kmask[:], in_=ktoks[:].unsqueeze(1).to_broadcast([NS, NS, 128]),
                                                     pattern=[[1, NS], [0, 128]], compare_op=ALU.is_equal, fill=0.0, base=0,
                                                     channel_multiplier=-1), [R_ktoks], [R_kmask])
            psum_release(bk)
            for g4 in range(4):
                sl_ = slice(g4 * 4, (g4 + 1) * 4)
                tt("pool", Sbig[:, hp, sl_, :], Sbig[:, hp, sl_, :], fS[:, sl_].unsqueeze(2).to_broadcast([128, 4, 128]), ALU.mult,
                   [R_Sbig[hp][g4], RfS], [R_Sbig[hp][g4]])

        def SR_main(u):
            h = u["h"]
            t, c0, n = 2, 1024, NS
            hp = h % 2
            hg = h // 4
            hc = slice(h * 128, (h + 1) * 128)
            s0 = 8 * hp
            fS, kS, lgs = smp[:, s0 + 4, :], smp[:, s0 + 5, :], smp[:, s0 + 7, :]
            RfS, RkS, Rlg = R_smp[s0 + 4], R_smp[s0 + 5], R_smp[s0 + 7]
            b_os = psum_alloc()

            def kv_pair(pr):
                bl = []
                for gg in (2 * pr, 2 * pr + 1):
                    bb = psum_alloc()
                    bl.append(bb)
                    for i in range(4):
                        mm(ps[:, bb, i * 128:(i + 1) * 128], kmask[:, gg * 4 + i, :], vtok[0:NS, 8, hc], True, True,
                           [R_kmask, R_v[(hg, 8)]], [R_ps[bb]])
                return bl

            def upd_pair(pr, bl):
                for gi, gg in enumerate((2 * pr, 2 * pr + 1)):
                    m0 = gg * 4
                    bb2 = bl[gi]
                    tt("dve", Sbig[:, hp, m0:m0 + 4, :], ps[:, bb2, :].rearrange("p (n v) -> p n v", v=128),
                       Sbig[:, hp, m0:m0 + 4, :], ALU.add, [R_ps[bb2], R_Sbig[hp][gg]], [R_Sbig[hp][gg]])
                    psum_release(bb2)
                    cp("act", Sb16[:, gg % 2], Sbig[:, hp, m0:m0 + 4, :], [R_Sbig[hp][gg]], [R_Sb16[gg % 2]])

            def o_pair(pr):
                for gg in (2 * pr, 2 * pr + 1):
                    m0 = gg * 4
                    for i in range(4):
                        nn = m0 + i
                        mm(ps[:, b_os, nn:nn + 1], Sb16[:, gg % 2, i, :], qS[:, hp, nn:nn + 1], True, True,
                           [R_Sb16[gg % 2], R_qS[hp]], [R_ps[b_os]])

            bl0 = kv_pair(0)
            upd_pair(0, bl0)
            bl1 = kv_pair(1)
            o_pair(0)
            upd_pair(1, bl1)
            o_pair(1)
            for g4 in range(4):
                dma("sp", hs[g4 * 4:(g4 + 1) * 4, h].rearrange("n d v -> d n v"), Sbig[:, hp, g4 * 4:(g4 + 1) * 4, :],
                    [R_Sbig[hp][g4]], [], "sbo%d_%d" % (hp, g4), is_out=True)
            norm_gate(h, t, c0, n, b_os, lgs, Rlg, 1)
            psum_release(b_os)

        def hgrn_phase(seg):
            rmsnorm(seg, C_HN, uT, R_u)
            vproj(seg)
            units = []
            for h in range(8):
                units.append({"kind": "p", "h": h, "t": 0, "c0": 0})
                units.append({"kind": "p", "h": h, "t": 1, "c0": 512})
            for pi, u in enumerate(units):
                u["ui"] = pi
                u["p"] = pi % 2
            head_slots = {}
            NU = len(units)
            for i in range(NU + 4):
                uP = units[i] if i < NU else None
                uE = units[i - 1] if 0 <= i - 1 < NU else None
                uR1 = units[i - 2] if 0 <= i - 2 < NU else None
                uR2 = units[i - 3] if 0 <= i - 3 < NU else None
                sP = uP["h"] if (seg == 0 and uP is not None and uP["t"] == 1) else None
                sR = uR2["h"] if (seg == 0 and uR2 is not None and uR2["t"] == 1) else None
                late_p = (seg == 1)
                if i == 0 and late_p:
                    head_slots[0] = (w_get(), w_get(), w_get())
                    uP["slots"] = head_slots[0]
                    P_pe(uP)
                if uR1 is not None:
                    R_early(uR1)
                if sR is not None:
                    SR_stage({"h": sR})
                if uP is not None and not late_p:
                    if uP["h"] not in head_slots:
                        head_slots[uP["h"]] = (w_get(), w_get(), w_get())
                    uP["slots"] = head_slots[uP["h"]]
                    P_pe(uP)
                if uE is not None:
                    E1(uE)
                if uP is not None:
                    P_act(uP)
                if uR1 is not None:
                    R_mid(uR1)
                if uE is not None:
                    E_act(uE)
                if uR2 is not None:
                    R_chain(uR2)
                if uP is not None:
                    P_dve(uP)
                if uE is not None:
                    E2(uE)
                if sP is not None:
                    su = {"h": sP, "slots": head_slots[sP]}
                    SP_stage(su)
                    SP_rest(su)
                if uP is not None and uP["t"] == 1:
                    for sl in head_slots[uP["h"]]:
                        w_release(sl)
                if uR2 is not None:
                    R_norm(uR2)
                if sR is not None:
                    SR_main({"h": sR})
                if late_p and i + 1 < NU:
                    uN = units[i + 1]
                    if uN["h"] not in head_slots:
                        head_slots[uN["h"]] = (w_get(), w_get(), w_get())
                    uN["slots"] = head_slots[uN["h"]]
                    P_pe(uN)
            for m in range(8):
                sl = w_get()
                for (t, c0, n) in seg_tiles(seg):
                    b = psum_next()
                    for h in range(8):
                        mm(ps[:, b, 0:n], ring[:, sl, h, :], oT[:, h, c0:c0 + n], h == 0, h == 7,
                           [R_slot[sl], R_o[(h, t)]], [R_ps[b]])
                    tt("dve", xT[:, m, c0:c0 + n], ps[:, b, 0:n], xT[:, m, c0:c0 + n], ALU.add,
                       [R_ps[b], R_x[(m, t)]], [R_x[(m, t)]])
                w_release(sl)

        def mlp_phase(seg, l):
            rmsnorm(seg, C_MN0 if l == 0 else C_MN1, uT, R_u)
            if seg == 0 and l == 0:
                build_bands()
            ri = 0
            for fb in range(2):
                for m in range(16):
                    sl = w_get()
                    for (t, c0, n) in seg_tiles(seg):
                        b = psum_next()
                        for k in range(8):
                            mm(ps[:, b, 0:n], ring[:, sl, k, :], uT[:, k, c0:c0 + n], k == 0, k == 7,
                               [R_slot[sl], R_u[(k, t)]], [R_ps[b]])
                        r = gt[:, ri % 2, 0:n]
                        Rr = R_gt[ri % 2]
                        ri += 1
                        act(r, ps[:, b, 0:n], AF.Relu, [R_ps[b]], [Rr])
                        tt("pool", hT[:, m, c0:c0 + n], r, r, ALU.mult, [Rr], [R_h[(m, t)]])
                    w_release(sl)
                for m in range(8):
                    sl0 = w_get()
                    sl1 = w_get()
                    for (t, c0, n) in seg_tiles(seg):
                        b = psum_next()
                        for k in range(16):
                            sl = sl0 if k < 8 else sl1
                            mm(ps[:, b, 0:n], ring[:, sl, k % 8, :], hT[:, k, c0:c0 + n], k == 0, k == 15,
                               [R_slot[sl], R_h[(k, t)]], [R_ps[b]])
                        tt("dve", xT[:, m, c0:c0 + n], ps[:, b, 0:n], xT[:, m, c0:c0 + n], ALU.add,
                           [R_ps[b], R_x[(m, t)]], [R_x[(m, t)]])
                    w_release(sl0)
                    w_release(sl1)

        def pool_phase(seg):
            for (t, c0, n) in seg_tiles(seg):
                b = psum_next()
                for q4 in range(2):
                    k0 = 4 * q4
                    g0 = 2 + 2 * q4
                    sqb = gt[:, g0:g0 + 2, :].rearrange("p a b -> p (a b)").bitcast(BF16).rearrange("p (k c) -> p k c", c=512)
                    Rsq = [R_gt[g0], R_gt[g0 + 1]]
                    act(sqb[:, :, 0:n], xT[:, k0:k0 + 4, c0:c0 + n], AF.Square, [R_x[(k0 + i, t)] for i in range(4)], Rsq)
                    for i in range(4):
                        k = k0 + i
                        mm(ps[:, b, 0:n], onesb[:], sqb[:, i, 0:n], k == 0, k == 7, [R_const] + Rsq, [R_ps[b]])
                gi = (12, 13, 6)[t]
                rstd = gt[:, gi, 0:n]
                act(rstd, ps[:, b, 0:n], AF.Ln, [R_ps[b]], [R_gt[gi]], bias=EPSB[:], scale=1.0 / D)
                act(rstd, rstd, AF.Exp, [R_gt[gi]], [R_gt[gi]], scale=-0.5)
                for k in range(8):
                    if t < 2:
                        dst = uT[:, k, c0:c0 + n]
                        Rd = R_u[(k, t)]
                    else:
                        dst = ups[:, k, :]
                        Rd = R_ups
                    stt(dst, xT[:, k, c0:c0 + n], pcol[:, C_PN + k: C_PN + k + 1], rstd, ALU.mult, ALU.mult,
                        [R_x[(k, t)], R_gt[gi], R_pcol], [Rd])
                if seg == 1 and t == 1:
                    for k in range(8):
                        stt(ups[:, k, 0:15], xT[:, k, SEGP - 15:SEGP], pcol[:, C_PN + k: C_PN + k + 1], rstd[:, 512 - 15:512],
                            ALU.mult, ALU.mult, [R_x[(k, 1)], R_gt[gi], R_pcol], [R_ups])
                    b = psum_next()
                    b2 = psum_next()
                    for k in range(8):
                        bb = b if k < 4 else b2
                        tr(ps[0:15, bb, (k % 4) * 128:(k % 4 + 1) * 128], ups[:, k, 0:15], ident[:],
                           [R_ups, R_const], [R_ps[bb]])
                    cp("act", stage[0:15, 0, 0:512], ps[0:15, b, :], [R_ps[b]], [R_stage[0]])
                    cp("dve", stage[0:15, 0, 512:1024], ps[0:15, b2, :], [R_ps[b2]], [R_stage[0]])
                    dma("sp", pp, stage[0:15, 0, :], [R_stage[0]], [], "stg0", is_out=True)
            if seg == 0:
                pool_samples_load()
            slots = [w_get() for _ in range(8)]
            Y1 = Sbig[:].rearrange("p a n v -> p (a n v)").bitcast(BF16).rearrange("p (j e) -> p j e", e=D)
            fence([r for rr in R_Sbig for r in rr], R_utok)
            for j in range(8):
                t = j // 4
                b = psum_next()
                b2 = psum_next()
                for m in range(8):
                    bb = b if m < 4 else b2
                    g = m // 2
                    for kk in range(2):
                        mm(ps[:, bb, (m % 4) * 128:(m % 4 + 1) * 128], uT[:, 2 * g + kk, j * 128:(j + 1) * 128],
                           ring[:, slots[m], kk, :], kk == 0, kk == 1, [R_u[(2 * g + kk, t)], R_slot[slots[m]]], [R_ps[bb]])
                cp("act", Y1[:, j, 0:512], ps[:, b, :], [R_ps[b]], [R_utok[j]])
                cp("dve", Y1[:, j, 512:1024], ps[:, b2, :], [R_ps[b2]], [R_utok[j]])
            for m in (4, 5, 6, 7, 0, 1, 2, 3):
                wi = m // 2
                msl = slice(m * 128, (m + 1) * 128)
                for t in range(2):
                    bz = psum_next()
                    for c in range(4):
                        j = t * 4 + c
                        cs_ = slice(c * 128, (c + 1) * 128)
                        first = True
                        if j > 0:
                            mm(ps[:, bz, cs_], Y1[:, j - 1, msl], bandc[:, 4 + wi, :], True, False,
                               [R_utok[j - 1], R_band], [R_ps[bz]])
                            first = False
                        elif seg == 1:
                            mm(ps[:, bz, cs_], utokprev[:, msl], bandc[:, 4 + wi, :], True, False,
                               [R_utokprev, R_band], [R_ps[bz]])
                            first = False
                        bm = bandc[:, 8 + wi, :] if (seg == 0 and j == 0) else bandc[:, wi, :]
                        mm(ps[:, bz, cs_], Y1[:, j, msl], bm, first, True, [R_utok[j], R_band], [R_ps[bz]])
                    c0 = t * 512
                    stt(xT[:, m, c0:c0 + 512], ps[:, bz, :], pcol[:, C_PS + m: C_PS + m + 1], xT[:, m, c0:c0 + 512],
                        ALU.mult, ALU.add, [R_ps[bz], R_pcol, R_x[(m, t)]], [R_x[(m, t)]])
            if seg == 0:
                cp("pool", utokprev[:], Y1[:, 7, :], [R_utok[7]], [R_utokprev])
                pool_samples_compute()
                t, c0, n = 2, 1024, NS
                for m in range(8):
                    sl = slots[m]
                    g = m // 2
                    b = psum_next()
                    for kk in range(2):
                        mm(ps[:, b, 0:n], ring[:, sl, kk, :], uT[:, 2 * g + kk, c0:c0 + n], kk == 0, kk == 1,
                           [R_slot[sl], R_u[(2 * g + kk, t)]], [R_ps[b]])
                    stt(xT[:, m, c0:c0 + n], ps[:, b, 0:n], pcol[:, C_PS + m: C_PS + m + 1], xT[:, m, c0:c0 + n],
                        ALU.mult, ALU.add, [R_ps[b], R_pcol, R_x[(m, t)]], [R_x[(m, t)]])
            for sl in slots:
                w_release(sl)

        R_ups = Res("ups")
        R_reds = Res("reds")

        def histT_of(half):
            return gt[:, 8 + 2 * half: 10 + 2 * half, :].rearrange("p a b -> p (a b)")[:, 0:8 * 120].rearrange("p (k r) -> p k r", r=120)

        def pool_samples_load():
            for half in range(2):
                Rh = [R_gt[8 + 2 * half], R_gt[9 + 2 * half]]
                histT = histT_of(half)
                dma("sp", stage[0:120, half, :], spl[half * 8: half * 8 + 8].rearrange("n j e -> (n j) e"), [],
                    [R_stage[half]], "stg%d" % half)
                dma("sp", pps[half * 8: half * 8 + 8, 0:14, :], spl[half * 8: half * 8 + 8, 1:15, :], [], [],
                    "ppsc%d" % half, is_out=True)
                b = psum_next()
                b2 = psum_next()
                for k in range(8):
                    bb = b if k < 4 else b2
                    tr(ps[:, bb, (k % 4) * 128:(k % 4) * 128 + 120], stage[0:120, half, k * 128:(k + 1) * 128],
                       ident[0:120, 0:120], [R_stage[half], R_const], [R_ps[bb]])
                cp("act", histT[:, 0:4, :], ps[:, b, :].rearrange("p (k c) -> p k c", c=128)[:, :, 0:120], [R_ps[b]], Rh)
                cp("act", histT[:, 4:8, :], ps[:, b2, :].rearrange("p (k c) -> p k c", c=128)[:, :, 0:120], [R_ps[b2]], Rh)

        def pool_samples_compute():
            for half in range(2):
                Rh = [R_gt[8 + 2 * half], R_gt[9 + 2 * half]]
                histT = histT_of(half)
                for kg in range(4):
                    w = WINDOWS[kg]
                    k0 = 2 * kg
                    hv = histT[:, k0:k0 + 2, :].rearrange("p k (n j) -> p k n j", j=15)
                    rd = reds[:, 0:16].rearrange("p (k n) -> p k n", n=8)
                    S.add("dve", lambda e, hv=hv, w=w, rd=rd: e.tensor_reduce(out=rd, in_=hv[:, :, :, 15 - (w - 1):15],
                                                                              axis=AX.X, op=ALU.add), Rh, [R_reds])
                    us = ups[:, k0:k0 + 2, half * 8: half * 8 + 8]
                    tt("dve", rd, rd, us, ALU.add, [R_reds, R_ups], [R_reds])
                    stt(uT[:, k0:k0 + 2, 1024 + half * 8: 1024 + half * 8 + 8], rd, 1.0 / w, us, ALU.mult, ALU.subtract,
                        [R_reds, R_ups], [R_u[(k0, 2)], R_u[(k0 + 1, 2)]])
            b = psum_next()
            b2 = psum_next()
            for k in range(8):
                bb = b if k < 4 else b2
                tr(ps[0:NS, bb, (k % 4) * 128:(k % 4 + 1) * 128], ups[:, k, :], ident[:], [R_ups, R_const], [R_ps[bb]])
            cp("act", stage[0:NS, 0, 0:512], ps[0:NS, b, :], [R_ps[b]], [R_stage[0]])
            cp("act", stage[0:NS, 0, 512:1024], ps[0:NS, b2, :], [R_ps[b2]], [R_stage[0]])
            dma("sp", pps[:, 14, :], stage[0:NS, 0, :], [R_stage[0]], [], "stg0", is_out=True)

        def y_chunks(seg):
            chunks = [(j, yp[seg * SEGP + j * 128: seg * SEGP + (j + 1) * 128, :], 128) for j in range(8)]
            if seg == 0:
                chunks.append((8, ys, NS))
            return chunks

        def final_out_chunk(seg, ci):
            (j, dst, rows) = y_chunks(seg)[ci]
            sg = ci % 2
            t = j // 4 if j < 8 else 2
            c0 = j * 128
            for half in range(2):
                b = psum_next()
                for kk in range(4):
                    k = half * 4 + kk
                    tr(ps[0:rows, b, kk * 128:(kk + 1) * 128], yT[:, k, c0:c0 + rows], ident[:],
                       [R_y[(k, t)], R_const], [R_ps[b]])
                eng = "act" if half == 0 else "dve"
                cp(eng, stage[0:rows, sg, half * 512:(half + 1) * 512], ps[0:rows, b, :], [R_ps[b]], [R_stage[sg]])
            dma("sp", dst, stage[0:rows, sg, :], [R_stage[sg]], [], "stg%d" % sg, is_out=True)

        Rv_all = list(R_v.values()) + list(R_o.values())
        Rh_all = list(R_h.values())
        Rup_all = list(R_up.values()) + R_uph
        Ry_all = list(R_y.values())
        R_prows = [Res("prow%d" % i) for i in range(9)]
        plist = [hgrn_norm, mlp_norm[0:8, :], mlp_norm[8:16, :], pool_norm, pool_scale, final_norm, hgrn_onorm,
                 hgrn_lb[0:8, :], hgrn_lb[8:16, :]]
        for i, p in enumerate(plist):
            dma("act", prow[8 * i:8 * i + 8, :], p, [], [R_prows[i]], "prow%d" % i)
        xl["bufs"] = [(xstage, 0, R_xs[0], "xs0"), (xstage, 1, R_xs[1], "xs1"), (xstage, 2, R_xs[2], "xs2"),
                      (stage, 0, R_stage[0], "stg0"), (stage, 1, R_stage[1], "stg1")]
        for ci in range(5):
            issue_load(0, ci)
        for ci in range(len(x_chunks(0))):
            load_chunk(0, ci)
            issue_load(0, ci + 5)
        xl["bufs"] = xl["bufs"][0:3]
        xl["n"] = 0
        b0 = psum_next()
        tr(ps[:, b0, 0:72], prow[0:72, :], ident[0:72, 0:72], R_prows + [R_const], [R_ps[b0]])
        cp("dve", pcol[:], ps[:, b0, 0:72], [R_ps[b0]], [R_pcol])
        C_HN, C_MN0, C_MN1, C_PN, C_PS, C_FN, C_ON, C_LB0, C_LB1 = [8 * i for i in range(9)]
        tt("dve", lbt[:, 0, :], pcol[:, C_LB1:C_LB1 + 8], pcol[:, C_LB0:C_LB0 + 8], ALU.subtract, [R_pcol], [R_misc])
        act(lbt[:, 1, :], lbt[:, 0, :], AF.Exp, [R_misc], [R_misc])
        ts("dve", lbt[:, 2, :], lbt[:, 1, :], 1.0, ALU.add, [R_misc], [R_misc])
        S.add("dve", lambda e: e.reciprocal(out=lbt[:, 3, :], in_=lbt[:, 2, :]), [R_misc], [R_misc])
        act(lbt[:, 4, :], lbt[:, 2, :], AF.Ln, [R_misc], [R_misc])
        tt("dve", lbt[:, 4, :], lbt[:, 0, :], lbt[:, 4, :], ALU.subtract, [R_misc], [R_lb, R_misc])
        LB = lambda h: lbt[:, 3, h:h + 1]
        LN1MLB = lambda h: lbt[:, 4, h:h + 1]

        def build_bands():
            Vt, GE0, T1, LT, BD, CSW = [gt[:, i, 0:128] for i in range(8, 14)]
            Rb_ = [R_gt[i] for i in range(8, 14)]
            S.add("pool", lambda e: e.iota(bandi[:], pattern=[[1, 128]], base=0, channel_multiplier=-1), [], [R_misc])
            cp("dve", Vt, bandi[:], [R_misc], [Rb_[0]])
            S.add("pool", lambda e: e.iota(bandi[:], pattern=[[1, 128]], base=1, channel_multiplier=0), [Rb_[0]], [R_misc])
            cp("dve", T1, bandi[:], [R_misc], [Rb_[2]])
            ts("dve", GE0, Vt, 0.0, ALU.is_ge, [Rb_[0]], [Rb_[1]])
            for wi, w in enumerate(WINDOWS):
                ts("dve", LT, Vt, w - 0.5, ALU.is_lt, [Rb_[0]], [Rb_[3]])
                tt("dve", BD, LT, GE0, ALU.mult, [Rb_[3], Rb_[1]], [Rb_[4]])
                stt(bandc[:, wi, :], BD, 1.0 / w, ident[:], ALU.mult, ALU.subtract, [Rb_[4], R_const], [R_band])
                ts("dve", bandc[:, 4 + wi, :], Vt, w - 128.5, ALU.is_lt, [Rb_[0]], [R_band], s2=1.0 / w, op1=ALU.mult)
                ts("dve", CSW, T1, float(w), ALU.min, [Rb_[2]], [Rb_[5]])
                S.add("dve", lambda e: e.reciprocal(out=CSW, in_=CSW), [Rb_[5]], [Rb_[5]])
                tt("dve", LT, BD, CSW, ALU.mult, [Rb_[4], Rb_[5]], [Rb_[3]])
                tt("dve", bandc[:, 8 + wi, :], LT, ident[:], ALU.subtract, [Rb_[3], R_const], [R_band])
            S.add("dve", lambda e: e.memset(utokprev[:], 0.0), [], [R_utokprev])
        for seg in range(2):
            hgrn_phase(seg)
            fence(Rv_all, Rh_all)
            mlp_phase(seg, 0)
            fence(Rh_all, Rup_all)
            pool_phase(seg)
            fence(Rup_all, Rh_all)
            mlp_phase(seg, 1)
            fence(Rh_all, Ry_all)
            if seg == 0:
                for ci in range(3):
                    issue_load(1, ci)
            rmsnorm(seg, C_FN, yT, R_y)
            nin = len(x_chunks(1)) if seg == 0 else 0
            for ci in range(len(y_chunks(seg))):
                final_out_chunk(seg, ci)
                if ci < nin:
                    load_chunk(1, ci)
                    issue_load(1, ci + 3)
            fence(Ry_all, Rv_all)
        dma("sp", hp.rearrange("h d v -> d h v"), Sst[:], R_S, [], "hpout", is_out=True)
        fin = S.add("sp", None, [], [])
        fin.deps = set(out_dma_ops)
        assert wstate["next_get"] == len(pieces), (wstate, len(pieces))
        S.emit(nc, stack, block)
    return nc


def kernel(x_prompt, x_sample, state_hgrn, state_pool, hgrn_norm, hgrn_w_in, hgrn_lb, hgrn_onorm,
           hgrn_w_out, pool_norm, pool_w, pool_scale, mlp_norm, mlp_up, mlp_down, final_norm):
    f = lambda a: np.ascontiguousarray(np.asarray(a, dtype=np.float32))
    shared = {
        "hgrn_norm": f(hgrn_norm).reshape(8, 128),
        "w_in": f(hgrn_w_in).reshape(D, 4 * D),
        "hgrn_lb": f(hgrn_lb).reshape(16, 128),
        "hgrn_onorm": f(hgrn_onorm).reshape(8, 128),
        "w_out": f(hgrn_w_out).reshape(D, D),
        "pool_norm": f(pool_norm).reshape(8, 128),
        "pool_w": f(pool_w).reshape(4, 256, 256),
        "pool_scale": f(pool_scale).reshape(8, 128),
        "mlp_norm": f(mlp_norm).reshape(16, 128),
        "mlp_up": f(mlp_up),
        "mlp_down": f(mlp_down),
        "final_norm": f(final_norm).reshape(8, 128),
    }
    xpr = f(x_prompt)
    xsm = f(x_sample)
    shg = f(state_hgrn)
    spo = f(state_pool)
    in_maps = []
    for c in range(NCORES):
        m = dict(shared)
        m["xp"] = xpr[c]
        m["xs"] = np.ascontiguousarray(xsm[NS * c: NS * (c + 1), 0, :])
        m["sh"] = np.ascontiguousarray(shg[0, NS * c: NS * (c + 1)])
        m["spl"] = np.ascontiguousarray(spo[0, NS * c: NS * (c + 1)])
        in_maps.append(m)
    nc = build()
    res = run_bass_kernel_spmd(nc, in_maps, core_ids=list(range(NCORES)))
    rs = res.results
    y_prompt = np.stack([rs[c]["yp"] for c in range(NCORES)], axis=0).astype(np.float32)
    y_sample = np.concatenate([rs[c]["ys"] for c in range(NCORES)], axis=0).reshape(NCORES * NS, 1, D).astype(np.float32)
    hgrn_p = np.stack([rs[c]["hp"] for c in range(NCORES)], axis=0)[None].astype(np.float32)
    hgrn_s = np.concatenate([rs[c]["hs"] for c in range(NCORES)], axis=0)[None].astype(np.float32)
    pool_p = np.stack([rs[c]["pp"] for c in range(NCORES)], axis=0)[None].astype(np.float32)
    pool_s = np.concatenate([rs[c]["pps"] for c in range(NCORES)], axis=0)[None].astype(np.float32)
    return (y_prompt, y_sample, hgrn_p, hgrn_s, pool_p, pool_s)


if __name__ == "__main__":
    import time
    t0 = time.time()
    nc = build()
    print("built in", time.time() - t0)
```
